# Optimizing a Trainium2 kernel written in Bass

```python
import math
import jax, jax.numpy as jnp
from jax import lax
import numpy as np

D_MODEL = 1024
BATCH = 2
SEQ = 16384
DEPTH = 2

MIX_WIDTH = D_MODEL
S5_WIDTH = MIX_WIDTH // 2
S5_GROUP_DIM = 16
S5_GROUPS = S5_WIDTH // S5_GROUP_DIM
S5_STATE = 64
S5_DT_MIN = 1e-3
S5_DT_MAX = 1e-1
HG_WIDTH = MIX_WIDTH - S5_WIDTH
HG_HEADS = 4
HG_HEAD_DIM = HG_WIDTH // HG_HEADS
GLA_HEADS = 4
GLA_KEY_WIDTH = D_MODEL // 2
GLA_VALUE_WIDTH = D_MODEL
GLA_DK = GLA_KEY_WIDTH // GLA_HEADS
GLA_DV = GLA_VALUE_WIDTH // GLA_HEADS
GLA_GATE_RANK = 16
GLA_GATE_NORM = 16.0
CHUNK = 64
FFN_HIDDEN = -(-8 * D_MODEL // (3 * 256)) * 256
N_EVEN = (DEPTH + 1) // 2
N_ODD = DEPTH // 2
EVEN_IN = S5_WIDTH + 4 * HG_WIDTH
ODD_IN = 2 * GLA_KEY_WIDTH + 2 * GLA_VALUE_WIDTH
N_MOD = 6
EPS = 1e-6

kernel_name = "hybrid_s5_hgrn2_gla_adaln_block"


def rms_norm(x, g):
    xf = x.astype(jnp.float32)
    y = xf * lax.rsqrt(jnp.mean(xf * xf, axis=-1, keepdims=True) + EPS)
    return (y * g.astype(jnp.float32)).astype(x.dtype)


def modulate(h, shift, scale):
    return h * (1 + scale[:, None, :]) + shift[:, None, :]


def _linear_recurrence_combine(e_i, e_j):
    a_i, b_i = e_i
    a_j, b_j = e_j
    return a_j * a_i, a_j * b_i + b_j


def chunk_gated_linear(q, k, v, log_g):
    bsz, seq, heads, dk = q.shape
    dv = v.shape[-1]
    n_chunks = seq // CHUNK

    def to_chunks(t):
        t = t.astype(jnp.float32).reshape(bsz, n_chunks, CHUNK, heads, t.shape[-1])
        return t.transpose(1, 0, 3, 2, 4)

    causal = jnp.tril(jnp.ones((CHUNK, CHUNK), dtype=bool))

    def step(state, inp):
        qc, kc, vc, gc = inp
        cum = jnp.cumsum(gc, axis=2)
        o_inter = jnp.einsum('bhik,bhkv->bhiv', qc * jnp.exp(cum), state)
        diff = cum[:, :, :, None, :] - cum[:, :, None, :, :]
        decay = jnp.exp(jnp.where(causal[:, :, None], diff, -jnp.inf))
        scores = jnp.einsum('bhik,bhjk,bhijk->bhij', qc, kc, decay)
        o_intra = jnp.einsum('bhij,bhjv->bhiv', scores, vc)
        last = cum[:, :, -1:, :]
        new_state = (jnp.exp(last[:, :, 0, :])[..., None] * state
                     + jnp.einsum('bhjk,bhjv->bhkv', kc * jnp.exp(last - cum), vc))
        return new_state, o_inter + o_intra

    state0 = jnp.zeros((bsz, heads, dk, dv), jnp.float32)
    _, o = lax.scan(step, state0, (to_chunks(q), to_chunks(k), to_chunks(v), to_chunks(log_g)))
    return o.transpose(1, 0, 3, 2, 4).reshape(bsz, seq, heads, dv)


def s5_mixer(u, lam_re, lam_im, b_re, b_im, c_re, c_im, d_skip, log_step, w_glu, b_glu):
    f32 = jnp.float32
    bsz, seq, _ = u.shape
    uf = u.astype(f32).reshape(bsz, seq, S5_GROUPS, S5_GROUP_DIM)
    lam = lax.complex(lam_re.astype(f32), lam_im.astype(f32))
    delta = jnp.exp(log_step.astype(f32))[:, None]
    lam_bar = jnp.exp(lam * delta)
    b_mat = lax.complex(b_re.astype(f32), b_im.astype(f32))
    b_bar = ((lam_bar - 1.0) / lam)[:, :, None] * b_mat
    bu = jnp.einsum('gph,blgh->blgp', b_bar, uf)
    a = jnp.broadcast_to(lam_bar, (1, seq, S5_GROUPS, S5_STATE))
    _, states = lax.associative_scan(_linear_recurrence_combine, (a, bu), axis=1)
    c_mat = lax.complex(c_re.astype(f32), c_im.astype(f32))
    y = (jnp.einsum('ghp,blgp->blgh', c_mat, states).real
         + d_skip.astype(f32).reshape(S5_GROUPS, S5_GROUP_DIM) * uf)
    y = jax.nn.gelu(y.reshape(bsz, seq, S5_WIDTH))
    return y * jax.nn.sigmoid(y @ w_glu.astype(f32) + b_glu.astype(f32))


def hgrn2_mixer(q_raw, f_raw, i_raw, g_raw, lower_bound, norm_g):
    f32 = jnp.float32
    bsz, seq, _ = q_raw.shape
    shp = (bsz, seq, HG_HEADS, HG_HEAD_DIM)
    f_pre = f_raw.astype(f32)
    lb = lower_bound.astype(f32)
    forget = lb + (1 - lb) * jax.nn.sigmoid(f_pre)
    key = (1 - lb) * jax.nn.sigmoid(-f_pre)
    query = jax.nn.silu(q_raw.astype(f32))
    o = chunk_gated_linear(query.reshape(shp), key.reshape(shp), i_raw.reshape(shp),
                           jnp.log(forget).reshape(shp))
    o = rms_norm(o, norm_g) * jax.nn.silu(g_raw.astype(f32).reshape(shp))
    return o.reshape(bsz, seq, HG_WIDTH)


def gla_mixer(h, w_in, w_a1, w_a2, b_a, norm_g):
    f32 = jnp.float32
    bsz, seq, _ = h.shape
    kshp = (bsz, seq, GLA_HEADS, GLA_DK)
    vshp = (bsz, seq, GLA_HEADS, GLA_DV)
    z = h @ w_in
    q, k, v, r = jnp.split(z, [GLA_KEY_WIDTH, 2 * GLA_KEY_WIDTH, 2 * GLA_KEY_WIDTH + GLA_VALUE_WIDTH], axis=-1)
    log_a = jax.nn.log_sigmoid(((h @ w_a1) @ w_a2 + b_a).astype(f32)) / GLA_GATE_NORM
    o = chunk_gated_linear(q.astype(f32).reshape(kshp) * (GLA_DK ** -0.5), k.reshape(kshp),
                           v.reshape(vshp), log_a.reshape(kshp))
    o = rms_norm(o, norm_g) * jax.nn.silu(r.astype(f32).reshape(vshp))
    return o.reshape(bsz, seq, GLA_VALUE_WIDTH)


def swiglu(h, w1, w3, w2):
    return (jax.nn.silu(h @ w1) * (h @ w3)) @ w2


def setup_inputs(seed: int = 0) -> dict:
    key = jax.random.key(seed)
    keys = list(jax.random.split(key, 32))

    def nrm(shape, scale):
        return scale * jax.random.normal(keys.pop(), shape, jnp.float32)

    D = D_MODEL
    inp = {}
    inp['x'] = nrm((BATCH, SEQ, D), 1.0)
    inp['c'] = nrm((BATCH, D), 1.0)
    inp['ada_w'] = nrm((DEPTH, D, N_MOD * D), 0.5 * D ** -0.5)
    inp['ada_b'] = nrm((DEPTH, N_MOD * D), 0.02)
    inp['norm_mix_g'] = 1.0 + nrm((DEPTH, D), 0.02)
    inp['norm_ffn_g'] = 1.0 + nrm((DEPTH, D), 0.02)
    inp['ev_w_in'] = nrm((N_EVEN, D, EVEN_IN), D ** -0.5)
    inp['ev_w_out'] = nrm((N_EVEN, MIX_WIDTH, D), MIX_WIDTH ** -0.5)
    inp['s5_lam_re'] = -0.5 * (1.0 + nrm((N_EVEN, S5_GROUPS, S5_STATE), 0.01))
    n_idx = jnp.arange(S5_STATE, dtype=jnp.float32)
    inp['s5_lam_im'] = math.pi * n_idx * (1.0 + nrm((N_EVEN, S5_GROUPS, S5_STATE), 0.01))
    inp['s5_b_re'] = nrm((N_EVEN, S5_GROUPS, S5_STATE, S5_GROUP_DIM), (2 * S5_GROUP_DIM) ** -0.5)
    inp['s5_b_im'] = nrm((N_EVEN, S5_GROUPS, S5_STATE, S5_GROUP_DIM), (2 * S5_GROUP_DIM) ** -0.5)
    inp['s5_c_re'] = nrm((N_EVEN, S5_GROUPS, S5_GROUP_DIM, S5_STATE), S5_STATE ** -0.5)
    inp['s5_c_im'] = nrm((N_EVEN, S5_GROUPS, S5_GROUP_DIM, S5_STATE), S5_STATE ** -0.5)
    inp['s5_d'] = nrm((N_EVEN, S5_WIDTH), 1.0)
    inp['s5_log_step'] = jax.random.uniform(keys.pop(), (N_EVEN, S5_GROUPS), jnp.float32,
                                            minval=math.log(S5_DT_MIN), maxval=math.log(S5_DT_MAX))
    inp['s5_w_glu'] = nrm((N_EVEN, S5_WIDTH, S5_WIDTH), S5_WIDTH ** -0.5)
    inp['s5_b_glu'] = nrm((N_EVEN, S5_WIDTH), 0.02)
    inp['hg_lb_logits'] = nrm((DEPTH + 1, HG_WIDTH), 0.1)
    inp['hg_norm_g'] = 1.0 + nrm((N_EVEN, HG_HEAD_DIM), 0.02)
    inp['od_w_in'] = nrm((N_ODD, D, ODD_IN), D ** -0.5)
    inp['od_w_a1'] = nrm((N_ODD, D, GLA_GATE_RANK), D ** -0.5)
    inp['od_w_a2'] = nrm((N_ODD, GLA_GATE_RANK, GLA_KEY_WIDTH), GLA_GATE_RANK ** -0.5)
    inp['od_b_a'] = nrm((N_ODD, GLA_KEY_WIDTH), 0.02)
    inp['gla_norm_g'] = 1.0 + nrm((N_ODD, GLA_DV), 0.02)
    inp['od_w_out'] = nrm((N_ODD, GLA_VALUE_WIDTH, D), GLA_VALUE_WIDTH ** -0.5)
    inp['ffn_w1'] = nrm((DEPTH, D, FFN_HIDDEN), D ** -0.5)
    inp['ffn_w3'] = nrm((DEPTH, D, FFN_HIDDEN), D ** -0.5)
    inp['ffn_w2'] = nrm((DEPTH, FFN_HIDDEN, D), FFN_HIDDEN ** -0.5)
    inp['final_norm_g'] = 1.0 + nrm((D,), 0.02)
    return inp


def reference(x, c, ada_w, ada_b, norm_mix_g, norm_ffn_g, ev_w_in, ev_w_out,
              s5_lam_re, s5_lam_im, s5_b_re, s5_b_im, s5_c_re, s5_c_im, s5_d, s5_log_step,
              s5_w_glu, s5_b_glu, hg_lb_logits, hg_norm_g, od_w_in, od_w_a1, od_w_a2, od_b_a,
              gla_norm_g, od_w_out, ffn_w1, ffn_w3, ffn_w2, final_norm_g):
    lower_bounds = jnp.cumsum(jax.nn.softmax(hg_lb_logits.astype(jnp.float32), axis=0), axis=0)
    cond = jax.nn.silu(c)
    for layer in range(DEPTH):
        mod = cond @ ada_w[layer] + ada_b[layer]
        shift_m, scale_m, gate_m, shift_f, scale_f, gate_f = jnp.split(mod, N_MOD, axis=-1)
        h = modulate(rms_norm(x, norm_mix_g[layer]), shift_m, scale_m)
        if layer % 2 == 0:
            e = layer // 2
            z = h @ ev_w_in[e]
            u, hq, hf, hi, hg = jnp.split(
                z, [S5_WIDTH, S5_WIDTH + HG_WIDTH, S5_WIDTH + 2 * HG_WIDTH, S5_WIDTH + 3 * HG_WIDTH], axis=-1)
            y_a = s5_mixer(u, s5_lam_re[e], s5_lam_im[e], s5_b_re[e], s5_b_im[e], s5_c_re[e],
                           s5_c_im[e], s5_d[e], s5_log_step[e], s5_w_glu[e], s5_b_glu[e])
            y_b = hgrn2_mixer(hq, hf, hi, hg, lower_bounds[layer], hg_norm_g[e])
            mixed = jnp.concatenate([y_a, y_b], axis=-1).astype(x.dtype) @ ev_w_out[e]
        else:
            o = layer // 2
            y_c = gla_mixer(h, od_w_in[o], od_w_a1[o], od_w_a2[o], od_b_a[o], gla_norm_g[o])
            mixed = y_c.astype(x.dtype) @ od_w_out[o]
        x = x + gate_m[:, None, :] * mixed
        h = modulate(rms_norm(x, norm_ffn_g[layer]), shift_f, scale_f)
        x = x + gate_f[:, None, :] * swiglu(h, ffn_w1[layer], ffn_w3[layer], ffn_w2[layer])
    return rms_norm(x, final_norm_g)
```

```python
import contextlib
import numpy as np
import concourse.bass as bass
import concourse.mybir as mybir
from concourse.bass_utils import run_bass_kernel_spmd

F32 = mybir.dt.float32
BF16 = mybir.dt.bfloat16
ALU = mybir.AluOpType
AF = mybir.ActivationFunctionType


class Buf:
    __slots__ = ("name", "w", "r")

    def __init__(self, name):
        self.name = name
        self.w = None
        self.r = {}


class Prog:
    ENG = ("pe", "dve", "act", "pool", "sp")
    NSLOT = 6

    def __init__(self, nc, stack):
        self.nc = nc
        self.stack = stack
        self.h = {"pe": nc.tensor, "dve": nc.vector, "act": nc.scalar,
                  "pool": nc.gpsimd, "sp": nc.sync}
        self.sem = {e: stack.enter_context(nc.semaphore("s_" + e)) for e in self.ENG}
        self.cnt = {e: 0 for e in self.ENG}
        self.ops = {e: [] for e in self.ENG}
        self.known = {e: {} for e in self.ENG}
        self.semid = {}
        for e in self.ENG:
            self.semid[id(self.sem[e])] = self.sem[e]
        self.dsem = {}
        self.dcnt = {}
        self.dnext = {}
        for q in ("sp", "pool", "act"):
            self.dsem[q] = [stack.enter_context(nc.semaphore("d_%s%d" % (q, i)))
                            for i in range(self.NSLOT)]
            self.dcnt[q] = [0] * self.NSLOT
            self.dnext[q] = 0
        self.bufs = []
        self.ntile = 0

    def buf(self, name):
        b = Buf(name)
        self.bufs.append(b)
        return b

    def sb(self, shape, dt, name=None):
        self.ntile += 1
        name = "%s_%d" % (name or "t", self.ntile)
        t = self.stack.enter_context(self.nc.sbuf_tensor(name, list(shape), dt))
        t_b = self.buf(name)
        return t, t_b

    def ps(self, shape, dt=F32, name=None):
        self.ntile += 1
        name = "%s_%d" % (name or "p", self.ntile)
        t = self.stack.enter_context(self.nc.psum_tensor(name, list(shape), dt))
        return t, self.buf(name)

    def _deps(self, reads, writes):
        toks = []
        for b in reads:
            if b.w is not None:
                toks.append(b.w)
        for b in writes:
            if b.w is not None:
                toks.append(b.w)
            toks.extend(b.r.values())
        return toks

    def _waits(self, eng, toks):
        best = {}
        for (s, v) in toks:
            k = id(s)
            if v > best.get(k, (None, 0))[1]:
                best[k] = (s, v)
        out = []
        kn = self.known[eng]
        for k, (s, v) in best.items():
            if kn.get(k, 0) >= v:
                continue
            kn[k] = v
            out.append((s, v))
        return out

    def _commit(self, tok, reads, writes):
        for b in writes:
            b.w = tok
            b.r = {}
        for b in reads:
            b.r[id(tok[0])] = tok

    def op(self, eng, fn, reads=(), writes=()):
        toks = self._deps(reads, writes)
        if eng == "pe":
            toks = [t for t in toks if t[0] is not self.sem["pe"]]
        waits = self._waits(eng, toks)
        self.cnt[eng] += 1
        tok = (self.sem[eng], self.cnt[eng])
        self.ops[eng].append((waits, fn, self.sem[eng], 1))
        self._commit(tok, reads, writes)
        return tok

    def dma(self, q, out, in_, reads=(), writes=(), **kw):
        toks = self._deps(reads, writes)
        i = self.dnext[q]
        self.dnext[q] = (i + 1) % self.NSLOT
        s = self.dsem[q][i]
        if self.dcnt[q][i] > 0:
            toks.append((s, 16 * self.dcnt[q][i]))
        waits = self._waits(q, toks)
        self.dcnt[q][i] += 1
        tok = (s, 16 * self.dcnt[q][i])
        self.ops[q].append((waits, lambda e: e.dma_start(out=out, in_=in_, **kw), s, 16))
        self._commit(tok, reads, writes)
        return tok

    def raw(self, eng, fn, sem, inc, reads=(), writes=(), extra=()):
        toks = self._deps(reads, writes) + list(extra)
        waits = self._waits(eng, toks)
        self.ops[eng].append((waits, fn, sem, inc))

    def barrier(self):
        toks = [(self.sem[e], self.cnt[e]) for e in self.ENG if self.cnt[e] > 0]
        for q in self.dsem:
            for i in range(self.NSLOT):
                if self.dcnt[q][i] > 0:
                    toks.append((self.dsem[q][i], 16 * self.dcnt[q][i]))
        for e in self.ENG:
            waits = self._waits(e, toks)
            if waits:
                self.ops[e].append((waits, None, None, 0))
        for b in self.bufs:
            b.w = None
            b.r = {}
        self.bufs = []

    def emit(self):
        nc = self.nc
        with nc.Block() as block:
            def run(e, lst):
                for (waits, fn, sem, inc) in lst:
                    for (s, v) in waits:
                        e.wait_ge(s, v)
                    if fn is not None:
                        ins = fn(e)
                        if sem is not None:
                            ins.then_inc(sem, inc)

            @block.tensor
            def _(e):
                run(e, self.ops["pe"])

            @block.vector
            def _(e):
                run(e, self.ops["dve"])

            @block.scalar
            def _(e):
                run(e, self.ops["act"])

            @block.gpsimd
            def _(e):
                run(e, self.ops["pool"])

            @block.sync
            def _(e):
                run(e, self.ops["sp"])
        self.ops = {e: [] for e in self.ENG}


D = 1024
KC = 8
FH = 2816
JH = 22
EPS = 1e-6
NB = 256
FR = 128
TWO_PI = float(2 * np.pi)


def build(SEG=4096, stages=("L0", "F0", "L1", "F1"), dbg=False, QB="pool", GROUPS=((0, 1, 2, 3), (4, 5, 6, 7)), EVSTOP=9):
    GROUPS = [list(g) for g in GROUPS]
    nc = bass.Bass("TRN2", target_bir_lowering=False)
    NBLK = SEG // NB
    NTT = NB // 128
    NCH = NB // 64

    INSHAPE = {
        "x": [SEG, D],
        "cT": [128, 8],
        "ada_w": [2, D, 6 * D],
        "ada_bT": [128, 2, 48],
        "nmgT": [128, 2, 8],
        "nfgT": [128, 2, 8],
        "fngT": [128, 8],
        "ev_w_in": [D, 2560],
        "ev_w_out": [D, D],
        "lamre_gh": [128, 4, 64],
        "lamim_gh": [128, 4, 64],
        "lstep_gh": [128, 4],
        "bre_gh": [128, 4, 64],
        "bim_gh": [128, 4, 64],
        "lamre_pp": [128, 16],
        "lamim_pp": [128, 16],
        "lstep_pp": [128, 16],
        "cre_pp": [128, 16, 16],
        "cim_pp": [128, 16, 16],
        "s5_dT": [128, 4],
        "s5_w_glu": [512, 512],
        "s5_bgluT": [128, 4],
        "hg_lbT": [128, 3, 4],
        "hg_ngT": [128, 1],
        "od_w_in": [D, 3072],
        "od_w_a1": [D, 16],
        "od_w_a2": [16, 512],
        "od_baT": [128, 4],
        "gla_ngT": [128, 2],
        "od_w_out": [D, D],
        "ffn_w1": [2, D, FH],
        "ffn_w3": [2, D, FH],
        "ffn_w2": [2, FH, D],
        "ident": [128, 128],
        "smask": [128, 128],
        "rmask": [128, NB],
        "jvec": [128, FR + 1],
        "eye8": [128, 8],
        "cmask2": [128, 4, 8],
        "segm": [128, 3],
    }
    _ins = {}

    def IN(name):
        if name not in _ins:
            _ins[name] = nc.dram_tensor(name, list(INSHAPE[name]), F32, kind="ExternalInput").ap()
        return _ins[name]

    out = nc.dram_tensor("out", [SEG, D], F32, kind="ExternalOutput").ap()
    XT = nc.dram_tensor("XT", [8, 128, SEG], F32).ap()
    HT = nc.dram_tensor("HT", [8, 128, SEG], BF16).ap()
    YA = nc.dram_tensor("YA", [4, 128, SEG], BF16).ap()
    YAv = YA.rearrange("k p t -> p k t")
    XTv = XT.rearrange("k p t -> p k t")
    HTv = HT.rearrange("k p t -> p k t")
    EXW = 1028 + 64
    ex_in = nc.dram_tensor("ex_in", [128, EXW], F32)
    ex_out = nc.dram_tensor("ex_out", [4 * 128, EXW], F32)
    ex_in2 = nc.dram_tensor("ex_in2", [128, EXW], F32)
    ex_out2 = nc.dram_tensor("ex_out2", [4 * 128, EXW], F32)

    main = contextlib.ExitStack()
    main.__enter__()
    P = Prog(nc, main)
    cc_sem = main.enter_context(nc.semaphore("cc"))
    cc_count = [0]

    def MM(o, lhsT, rhs, start, stop, R, W):
        P.op("pe", lambda e: e.matmul(o, lhsT=lhsT, rhs=rhs, start=start, stop=stop), reads=R, writes=W)

    def ACT(o, i, func, R, W, scale=1.0, bias=None):
        if bias is None:
            P.op("act", lambda e: e.activation(out=o, in_=i, func=func, scale=scale), reads=R, writes=W)
        else:
            P.op("act", lambda e: e.activation(out=o, in_=i, func=func, scale=scale, bias=bias), reads=R, writes=W)

    def TT(eng, o, a, b, op, R, W):
        P.op(eng, lambda e: e.tensor_tensor(out=o, in0=a, in1=b, op=op), reads=R, writes=W)

    def TS(eng, o, a, s1, s2, op0, op1, R, W):
        if s2 is None:
            P.op(eng, lambda e: e.tensor_scalar(out=o, in0=a, scalar1=s1, scalar2=None, op0=op0), reads=R, writes=W)
        else:
            P.op(eng, lambda e: e.tensor_scalar(out=o, in0=a, scalar1=s1, scalar2=s2, op0=op0, op1=op1), reads=R, writes=W)

    def STT(eng, o, a, s, b, op0, op1, R, W):
        P.op(eng, lambda e: e.scalar_tensor_tensor(out=o, in0=a, scalar=s, in1=b, op0=op0, op1=op1), reads=R, writes=W)

    def CP(eng, o, i, R, W):
        if eng == "act":
            P.op("act", lambda e: e.activation(out=o, in_=i, func=AF.Copy), reads=R, writes=W)
        else:
            P.op(eng, lambda e: e.tensor_copy(out=o, in_=i), reads=R, writes=W)

    def SCAN(o, d0, d1, init, R, W):
        P.op("dve", lambda e: e.tensor_tensor_scan(out=o, data0=d0, data1=d1, initial=init,
                                                   op0=ALU.mult, op1=ALU.add), reads=R, writes=W)

    def MEMSET(eng, o, v, W):
        P.op(eng, lambda e: e.memset(o, v), writes=W)

    def RECIP(o, i, R, W):
        P.op("dve", lambda e: e.reciprocal(out=o, in_=i), reads=R, writes=W)

    ident, identb = P.sb([128, 128], F32, "ident")
    ones_bf, onesb = P.sb([128, 128], BF16, "ones")
    smask, smaskb = P.sb([128, 128], F32, "smask")
    rmask, rmaskb = P.sb([128, NB], F32, "rmask")
    cst, cstb = P.sb([128, 8], F32, "cst")
    modT, modb = P.sb([128, 2, 48], F32, "mod")
    gsm, gsmb = P.sb([128, 2, 8], F32, "gsm")
    gsf, gsfb = P.sb([128, 2, 8], F32, "gsf")
    fng, fngb = P.sb([128, 8], F32, "fng")
    segm, segmb = P.sb([128, 3], F32, "segm")

    PB = []
    PBb = []
    for i in range(4):
        t = main.enter_context(nc.psum_tensor("pb%d" % i, [128, 1024], F32))
        PB.append(t)
        PBb.append(P.buf("pbA%d" % i))
        PBb.append(P.buf("pbB%d" % i))

    def bank(i):
        return PB[i // 2][:, (i % 2) * 512:(i % 2) * 512 + 512], PBb[i]

    def phase_end():
        P.barrier()
        P.emit()

    with contextlib.ExitStack() as ps:
        P.stack = ps
        P.dma("sp", ident[:], IN("ident"), writes=[identb])
        P.dma("sp", smask[:], IN("smask"), writes=[smaskb])
        P.dma("sp", rmask[:], IN("rmask"), writes=[rmaskb])
        P.dma("sp", fng[:], IN("fngT"), writes=[fngb])
        P.dma("sp", segm[:], IN("segm"), writes=[segmb])
        MEMSET("pool", ones_bf[:], 1.0, [onesb])
        MEMSET("pool", cst[:, 0:1], EPS, [cstb])
        MEMSET("pool", cst[:, 1:2], 1.0, [cstb])
        MEMSET("pool", cst[:, 2:3], 0.0, [cstb])
        cT, cTb = P.sb([128, 8], F32, "cT")
        cond, condb = P.sb([128, 8], F32, "cond")
        adab, adabb = P.sb([128, 2, 48], F32, "adab")
        nmg, nmgb = P.sb([128, 2, 8], F32, "nmg")
        nfg, nfgb = P.sb([128, 2, 8], F32, "nfg")
        P.dma("sp", cT[:], IN("cT"), writes=[cTb])
        P.dma("sp", adab[:], IN("ada_bT"), writes=[adabb])
        P.dma("sp", nmg[:], IN("nmgT"), writes=[nmgb])
        P.dma("sp", nfg[:], IN("nfgT"), writes=[nfgb])
        ACT(cond[:], cT[:], AF.Silu, [cTb], [condb])
        wts = [P.sb([128, 8, 768], F32, "adaw") for _ in range(2)]
        pm, pmb = bank(0)
        gi = 0
        for layer in range(2):
            awv = IN("ada_w")[layer].rearrange("(k p) c -> p k c", p=128)
            for grp in range(8):
                wt, wtb = wts[gi % 2]
                for k in range(8):
                    P.dma("sp" if k % 2 == 0 else QB, wt[:, k, :], awv[:, k, grp * 768:(grp + 1) * 768], writes=[wtb])
                for mi in range(6):
                    m = grp * 6 + mi
                    col = layer * 48 + m
                    for k in range(8):
                        MM(pm[:, col:col + 1], wt[:, k, mi * 128:(mi + 1) * 128], cond[:, k:k + 1],
                           k == 0, k == 7, [wtb, condb], [pmb])
                gi += 1
            TT("dve", modT[:, layer, :], pm[:, layer * 48:(layer + 1) * 48], adab[:, layer, :], ALU.add,
               [pmb, adabb], [modb])
            STT("dve", gsm[:, layer, :], modT[:, layer, 8:16], 1.0, nmg[:, layer, :], ALU.add, ALU.mult,
                [modb, nmgb], [gsmb])
            STT("dve", gsf[:, layer, :], modT[:, layer, 32:40], 1.0, nfg[:, layer, :], ALU.add, ALU.mult,
                [modb, nfgb], [gsfb])
        phase_end()

    def norm_block(xt, xtb, hT, hTb, gs, shift, Rv, T, nbanks):
        n = xt.shape[2]
        sq, sqb = T["sq"]
        tmp, tmpb = T["tmp"]
        rs, rsb = T["rs"]
        pn, pnb = bank(nbanks)
        ACT(sq[:, :, 0:n], xt[:, :, :], AF.Square, [xtb], [sqb])
        for k in range(8):
            MM(pn[:, 0:n], ones_bf[:], sq[:, k, 0:n], k == 0, k == 7, [onesb, sqb], [pnb])
        ACT(rs[:, 0:n], pn[:, 0:n], AF.Sqrt, [pnb, cstb], [rsb], scale=1.0 / D, bias=cst[:, 0:1])
        RECIP(rs[:, 0:n], rs[:, 0:n], [rsb], [rsb])
        TT("dve", tmp[:, :, 0:n], xt[:, :, :], rs[:, 0:n].unsqueeze(1).to_broadcast([128, 8, n]), ALU.mult,
           [xtb, rsb], [tmpb])
        for k in range(8):
            if shift is None:
                ACT(hT[:, k, 0:n], tmp[:, k, 0:n], AF.Identity, [tmpb] + Rv, [hTb], scale=gs[:, k:k + 1],
                    bias=cst[:, 2:3])
            else:
                ACT(hT[:, k, 0:n], tmp[:, k, 0:n], AF.Identity, [tmpb] + Rv, [hTb], scale=gs[:, k:k + 1],
                    bias=shift[:, k:k + 1])

    def load_w(dst, dstb, src, kt, q="pool"):
        v = src.rearrange("(k p) c -> p k c", p=128)
        for k in range(kt):
            P.dma(q, dst[:, k, :], v[:, k, :], writes=[dstb])

    with contextlib.ExitStack() as ps:
        P.stack = ps
        XB = 512
        xins = [P.sb([128, 4, D], F32, "xin") for _ in range(2)]
        xTs = [P.sb([128, 8, XB], F32, "xT") for _ in range(2)]
        for blk in range(SEG // XB):
            t0 = blk * XB
            xin, xinb = xins[blk % 2]
            xT, xTb = xTs[blk % 2]
            P.dma("sp", xin[:], IN("x")[t0:t0 + XB, :].rearrange("(tt p) f -> p tt f", p=128), writes=[xinb])
            for k in range(8):
                pk, pkb = bank(k % 4)
                for tt in range(4):
                    P.op("pe", (lambda o, i: lambda e: e.transpose(o, i, ident[:]))(
                        pk[:, tt * 128:(tt + 1) * 128], xin[:, tt, k * 128:(k + 1) * 128]),
                        reads=[xinb, identb], writes=[pkb])
                CP("act" if k % 2 == 0 else "dve", xT[:, k, :], pk[:, :], [pkb], [xTb])
            P.dma("sp", XTv[:, :, t0:t0 + XB], xT[:], reads=[xTb])
        phase_end()

    def ffn_phase(l):
        FB = 512
        with contextlib.ExitStack() as ps:
            P.stack = ps
            w1, w1b = P.sb([128, 8, FH], BF16, "w1")
            w3, w3b = P.sb([128, 8, FH], BF16, "w3")
            w2, w2b = P.sb([128, JH, D], BF16, "w2")
            load_w(w1, w1b, IN("ffn_w1")[l], 8)
            load_w(w3, w3b, IN("ffn_w3")[l], 8)
            load_w(w2, w2b, IN("ffn_w2")[l], JH)
            xt, xtb0 = P.sb([128, 8, FB], F32, "xt")
            xtbs = [P.buf("xtk%d" % k) for k in range(8)]
            sq, sqb = P.sb([128, 8, FB], BF16, "sq")
            rs, rsb = P.sb([128, FB], F32, "rs")
            hT, hTb = P.sb([128, 8, FB], BF16, "hT")
            aT, aTb = P.sb([128, JH, FB], BF16, "aT")
            sils = [P.sb([128, FB], F32, "sil") for _ in range(2)]
            for blk in range(SEG // FB):
                t0 = blk * FB
                for k in range(8):
                    P.dma("sp", xt[:, k, :], XTv[:, k, t0:t0 + FB], writes=[xtbs[k]])
                pn, pnb = bank(6)
                ACT(sq[:], xt[:], AF.Square, xtbs, [sqb])
                for k in range(8):
                    MM(pn[:, 0:FB], ones_bf[:], sq[:, k, :], k == 0, k == 7, [onesb, sqb], [pnb])
                ACT(rs[:], pn[:, 0:FB], AF.Sqrt, [pnb, cstb], [rsb], scale=1.0 / D, bias=cst[:, 0:1])
                RECIP(rs[:], rs[:], [rsb], [rsb])
                TT("dve", aT[:, 0:8, :], xt[:], rs[:].unsqueeze(1).to_broadcast([128, 8, FB]), ALU.mult, xtbs + [rsb], [aTb])
                for k in range(8):
                    ACT(hT[:, k, :], aT[:, k, :], AF.Identity, [aTb, gsfb, modb], [hTb], scale=gsf[:, l, k:k + 1],
                        bias=modT[:, l, 24 + k:25 + k])
                for j in range(JH):
                    pa, pab = bank(j % 2)
                    pc, pcb = bank(2 + j % 2)
                    for k in range(8):
                        MM(pa[:, 0:FB], w1[:, k, j * 128:(j + 1) * 128], hT[:, k, :], k == 0, k == 7, [w1b, hTb], [pab])
                    for k in range(8):
                        MM(pc[:, 0:FB], w3[:, k, j * 128:(j + 1) * 128], hT[:, k, :], k == 0, k == 7, [w3b, hTb], [pcb])
                    sil, silb = sils[j % 2]
                    ACT(sil[:], pa[:, 0:FB], AF.Silu, [pab], [silb])
                    TT("dve", aT[:, j, :], sil[:], pc[:, 0:FB], ALU.mult, [silb, pcb], [aTb])
                for m in range(8):
                    po, pob = bank(4 + m % 2)
                    for j in range(JH):
                        MM(po[:, 0:FB], w2[:, j, m * 128:(m + 1) * 128], aT[:, j, :], j == 0, j == JH - 1, [w2b, aTb], [pob])
                    STT("dve", xt[:, m, :], po[:, 0:FB], modT[:, l, 40 + m:41 + m], xt[:, m, :], ALU.mult, ALU.add,
                        [pob, modb, xtbs[m]], [xtbs[m]])
                    P.dma("sp", XTv[:, m, t0:t0 + FB], xt[:, m, :], reads=[xtbs[m]])
            phase_end()

    def out_phase():
        with contextlib.ExitStack() as ps:
            P.stack = ps
            XB = 512
            xts = [P.sb([128, 8, XB], F32, "xt") for _ in range(2)]
            T = {"sq": P.sb([128, 8, XB], BF16, "sq"), "tmp": P.sb([128, 8, XB], F32, "tmp"),
                 "rs": P.sb([128, XB], F32, "rs")}
            yT, yTb = P.sb([128, 8, XB], F32, "yT")
            oks = [P.sb([128, 4, D], F32, "otok") for _ in range(2)]
            for blk in range(SEG // XB):
                t0 = blk * XB
                xt, xtb = xts[blk % 2]
                otok, otokb = oks[blk % 2]
                P.dma("sp", xt[:], XTv[:, :, t0:t0 + XB], writes=[xtb])
                norm_block(xt, xtb, yT, yTb, fng, None, [fngb, cstb], T, 6)
                ei = 0
                for tt in range(4):
                    for half in range(2):
                        pk, pkb = bank((tt * 2 + half) % 4)
                        for k4 in range(4):
                            k = half * 4 + k4
                            P.op("pe", (lambda o, i: lambda e: e.transpose(o, i, ident[:]))(
                                pk[:, k4 * 128:(k4 + 1) * 128], yT[:, k, tt * 128:(tt + 1) * 128]),
                                reads=[yTb, identb], writes=[pkb])
                        CP("act" if ei % 2 == 0 else "dve", otok[:, tt, half * 512:(half + 1) * 512], pk[:, :], [pkb], [otokb])
                        ei += 1
                P.dma("sp", out[t0:t0 + XB, :].rearrange("(tt p) f -> p tt f", p=128), otok[:], reads=[otokb])
            phase_end()

    def TR(o, i, R, W):
        P.op("pe", lambda e: e.transpose(o, i, ident[:]), reads=R + [identb], writes=W)

    def drive(gens):
        gens = [g for g in gens if g is not None]
        while gens:
            nxt = []
            for g in gens:
                try:
                    next(g)
                    nxt.append(g)
                except StopIteration:
                    pass
            gens = nxt

    def run(gen):
        for _ in gen:
            pass

    def recur_front(Tt, heads, dv, cs, full, trb=0):
        lgX, lgXb = Tt["lgX"]
        k32, k32b = Tt["k32"]
        cum, cumb = Tt["cum"]
        E, Eb = Tt["E"]
        kd32, kd32b = Tt["kd32"]
        ext, extb = Tt["ext"]
        kl32, kl32b = Tt["kl32"]
        for hd in range(heads):
            SCAN(cum[:, hd, :], rmask[:, 0:NB], lgX[:, hd, :], 0.0, [rmaskb, lgXb], [cumb])
            yield
        ACT(E[:], cum[:], AF.Exp, [cumb], [Eb], scale=-cs)
        yield
        TT("dve", kd32[:], k32[:], E[:], ALU.mult, [k32b, Eb], [kd32b])
        cum4 = cum[:].rearrange("p h (c j) -> p h c j", j=64)
        ACT(ext[:], cum4[:, :, :, 63], AF.Exp, [cumb], [extb], scale=cs)
        yield
        TT("pool", kl32[:].rearrange("p h (c j) -> p h c j", j=64),
           kd32[:].rearrange("p h (c j) -> p h c j", j=64),
           ext[:].unsqueeze(3).to_broadcast([128, heads, NCH, 64]), ALU.mult, [kd32b, extb], [kl32b])
        yield
        if full:
            qact, qactb = Tt["qact"]
            qd, qdb = Tt["qd"]
            kd, kdb = Tt["kd"]
            ACT(E[:], cum[:], AF.Exp, [cumb], [Eb], scale=cs)
            yield
            TT("dve", qd[:], qact[:], E[:], ALU.mult, [qactb, Eb], [qdb])
            CP("pool", kd[:], kd32[:], [kd32b], [kdb])
            yield
        for tt in range(NTT):
            tsl = slice(tt * 128, (tt + 1) * 128)
            kltok, kltokb = Tt["kltok"][tt]
            bT, bTb = bank(trb)
            for hd in range(heads):
                TR(bT[:, hd * 128:(hd + 1) * 128], kl32[:, hd, tsl], [kl32b], [bTb])
            CP("act", kltok[:], bT[:, 0:heads * 128], [bTb], [kltokb])
            yield
            if full:
                sT, sTb = Tt["sT"][tt]
                bS, bSb = bank(1)
                for hd in range(heads):
                    MM(bS[:, hd * 128:(hd + 1) * 128], kd[:, hd, tsl], qd[:, hd, tsl], True, True, [kdb, qdb], [bSb])
                TT("dve", sT[:], bS[:, 0:heads * 128].rearrange("p (h i) -> p h i", i=128),
                   smask[:].unsqueeze(1).to_broadcast([128, heads, 128]), ALU.mult, [bSb, smaskb], [sTb])
                yield

    def recur_back(Tt, St, heads, dv, cs, full):
        VH = dv // 128
        ext, extb = Tt["ext"]
        if full:
            qd, qdb = Tt["qd"]
            oT, oTb = Tt["oT"]
        for tt in range(NTT):
            tsl = slice(tt * 128, (tt + 1) * 128)
            kltok, kltokb = Tt["kltok"][tt]
            vtok, vtokb = Tt["vtok"][tt]
            if full:
                sT, sTb = Tt["sT"][tt]
            for c in range(2):
                chunk = tt * 2 + c
                csl = slice(c * 64, (c + 1) * 64)
                for hd in range(heads):
                    if full:
                        cur, curb = St["Sbf"][St["sidx"][hd]][hd]
                        for vh in range(VH):
                            bO, bOb = bank(2 + vh)
                            oreg = bO[:, hd * 128 + c * 64:hd * 128 + c * 64 + 64]
                            MM(oreg, cur[:, vh * 128:(vh + 1) * 128], qd[:, hd, tt * 128 + c * 64:tt * 128 + c * 64 + 64],
                               True, False, [curb, qdb], [bOb])
                            MM(oreg, vtok[csl, hd * dv + vh * 128:hd * dv + vh * 128 + 128], sT[csl, hd, csl],
                               False, True, [vtokb, sTb], [bOb])
                    bD, bDb = bank(4 + St["di"] % 2)
                    St["di"] += 1
                    S32, S32b = St["S32"][hd]
                    MM(bD[:, 0:dv], kltok[csl, hd * 128:(hd + 1) * 128], vtok[csl, hd * dv:(hd + 1) * dv], True, True,
                       [kltokb, vtokb], [bDb])
                    STT("dve", S32[:], S32[:], ext[:, hd, chunk:chunk + 1], bD[:, 0:dv], ALU.mult, ALU.add,
                        [S32b, extb, bDb], [S32b])
                    if full:
                        nxt = 1 - St["sidx"][hd]
                        nb_, nbb = St["Sbf"][nxt][hd]
                        CP("pool", nb_[:], S32[:], [S32b], [nbb])
                        St["sidx"][hd] = nxt
                    yield
            if full:
                oT4 = oT[:].rearrange("p (h v) n -> p h v n", v=VH)
                for vh in range(VH):
                    bO, bOb = bank(2 + vh)
                    CP("act", oT4[:, :, vh, tsl], bO[:, 0:heads * 128].rearrange("p (h i) -> p h i", i=128), [bOb], [oTb])
                yield

    def recur(Tt, St, heads, dv, cs, full):
        run(recur_front(Tt, heads, dv, cs, full))
        run(recur_back(Tt, St, heads, dv, cs, full))

    def post_norm(*a, **k):
        run(post_norm_g(*a, **k))

    def post_norm_g(Tt, heads, dv, ng, ngb, nb0=6):
        VH = dv // 128
        oT, oTb = Tt["oT"]
        sqo, sqob = Tt["sqo"]
        rso, rsob = Tt["rso"]
        tn, tnb = Tt["tn"]
        gact, gactb = Tt["gact"]
        if "yg_ap" in Tt:
            yg_ap, ygb = Tt["yg_ap"]
        else:
            yg_ap, ygb = Tt["yg"][0][:], Tt["yg"][1]
        ACT(sqo[:], oT[:], AF.Square, [oTb], [sqob])
        yield
        for hd in range(heads):
            bN, bNb = bank(nb0 + hd // 2)
            reg = bN[:, (hd % 2) * NB:(hd % 2) * NB + NB]
            for vh in range(VH):
                MM(reg, ones_bf[:], sqo[:, hd * VH + vh, :], vh == 0, vh == VH - 1, [onesb, sqob], [bNb])
        for pb in range(heads // 2):
            bN, bNb = bank(nb0 + pb)
            ACT(rso[:, 2 * pb:2 * pb + 2, :], bN[:, 0:2 * NB].rearrange("p (h n) -> p h n", n=NB), AF.Sqrt,
                [bNb, cstb], [rsob], scale=1.0 / dv, bias=cst[:, 0:1])
        yield
        RECIP(rso[:], rso[:], [rsob], [rsob])
        oT4 = oT[:].rearrange("p (h v) n -> p h v n", v=VH)
        tn4 = tn[:].rearrange("p (h v) n -> p h v n", v=VH)
        g4 = gact[:].rearrange("p (h v) n -> p h v n", v=VH)
        y4 = yg_ap.rearrange("p (h v) n -> p h v n", v=VH)
        TT("dve", tn4, oT4, rso[:].unsqueeze(2).to_broadcast([128, heads, VH, NB]), ALU.mult, [oTb, rsob], [tnb])
        yield
        for vh in range(VH):
            STT("dve", y4[:, :, vh, :], tn4[:, :, vh, :], ng[:, vh:vh + 1], g4[:, :, vh, :], ALU.mult, ALU.mult,
                [tnb, ngb, gactb], [ygb])

    def exchange(ex_i, ex_o, pack_fn, unpack_fn, width):
        with contextlib.ExitStack() as es:
            P.stack = es
            exs, exsb = P.sb([128, EXW], F32, "exs")
            exg, exgb = P.sb([128, 4, EXW], F32, "exg")
            MEMSET("pool", exs[:], 0.0, [exsb])
            pack_fn(exs, exsb)
            CW = 256
            nchunk = (EXW + CW - 1) // CW
            for ci_ in range(nchunk):
                c0 = ci_ * CW
                cw = min(CW, EXW - c0)
                di_ = nc.dram_tensor("exi_%d_%d" % (ex_i, ci_), [128, cw], F32)
                do_ = nc.dram_tensor("exo_%d_%d" % (ex_i, ci_), [4 * 128, cw], F32)
                eob = P.buf("exout")
                tk = P.dma("sp", di_.ap(), exs[:, c0:c0 + cw], reads=[exsb])
                cc_count[0] += 1
                P.raw("pool", (lambda a, b: lambda e: e.collective_compute(
                    "AllGather", ALU.bypass, replica_groups=GROUPS, ins=[a.ap()], outs=[b.ap()]))(di_, do_),
                    cc_sem, 1, extra=[tk])
                eob.w = (cc_sem, cc_count[0])
                P.dma("sp", exg[:, :, c0:c0 + cw], do_.ap().rearrange("(r p) c -> p r c", p=128), reads=[eob], writes=[exgb])
            unpack_fn(exg, exgb)
            phase_end()

    def combine_states(exg, exgb, St, heads, dv, cs, off_d):
        W = heads * dv
        Dr, Drb = P.sb([128, 3, heads], F32, "Dr")
        acc, accb = P.sb([128, heads, dv], F32, "acc")
        new, newb = P.sb([128, heads, dv], F32, "new")
        ACT(Dr[:], exg[:, 0:3, off_d:off_d + heads], AF.Exp, [exgb], [Drb], scale=cs)
        MEMSET("pool", acc[:], 0.0, [accb])
        for r in range(3):
            TT("dve", new[:], acc[:], Dr[:, r, :].unsqueeze(2).to_broadcast([128, heads, dv]), ALU.mult,
               [accb, Drb], [newb])
            TT("dve", new[:], new[:], exg[:, r, 0:W].rearrange("p (h v) -> p h v", v=dv), ALU.add, [newb, exgb], [newb])
            TT("dve", new[:], new[:], acc[:], ALU.subtract, [newb, accb], [newb])
            STT("dve", acc[:], new[:], segm[:, r:r + 1], acc[:], ALU.mult, ALU.add, [newb, segmb, accb], [accb])
        for hd in range(heads):
            S32, S32b = St["S32"][hd]
            CP("dve", S32[:], acc[:, hd, :], [accb], [S32b])
            sb0, sb0b = St["Sbf"][0][hd]
            CP("pool", sb0[:], acc[:, hd, :], [accb], [sb0b])
            St["sidx"][hd] = 0

    def new_state(heads, dv):
        St = {"S32": [P.sb([128, dv], F32, "S32") for _ in range(heads)],
              "Sbf": [[P.sb([128, dv], BF16, "Sbf") for _ in range(heads)] for _ in range(2)],
              "sidx": [0] * heads, "di": 0}
        for hd in range(heads):
            MEMSET("pool", St["S32"][hd][0][:], 0.0, [St["S32"][hd][1]])
        St["dsum"] = P.sb([128, heads], F32, "dsum")
        MEMSET("pool", St["dsum"][0][:], 0.0, [St["dsum"][1]])
        return St

    def recur_tiles(heads, dv, full):
        VH = dv // 128
        Tt = {k: P.sb([128, heads, NB], F32, k) for k in ("lgX", "k32", "cum", "E", "kd32", "kl32")}
        Tt["ext"] = P.sb([128, heads, NCH], F32, "ext")
        Tt["kltok"] = [P.sb([128, heads * 128], BF16, "kltok") for _ in range(NTT)]
        Tt["vtok"] = [P.sb([128, heads * dv], BF16, "vtok") for _ in range(NTT)]
        Tt["red"] = P.sb([128, heads], F32, "red")
        if full:
            Tt["qact"] = P.sb([128, heads, NB], F32, "qact")
            Tt["qd"] = P.sb([128, heads, NB], BF16, "qd")
            Tt["kd"] = P.sb([128, heads, NB], BF16, "kd")
            Tt["sT"] = [P.sb([128, heads, 128], BF16, "sT") for _ in range(NTT)]
            Tt["oT"] = P.sb([128, heads * VH, NB], F32, "oT")
            Tt["sqo"] = P.sb([128, heads * VH, NB], BF16, "sqo")
            Tt["rso"] = P.sb([128, heads, NB], F32, "rso")
            Tt["tn"] = P.sb([128, heads * VH, NB], F32, "tn")
            Tt["gact"] = P.sb([128, heads * VH, NB], BF16, "gact")
            Tt["yg"] = P.sb([128, heads * VH, NB], BF16, "yg")
        return Tt

    def alt_tiles(base, heads, dv, full):
        alt = dict(base)
        alt["ext"] = P.sb([128, heads, NCH], F32, "ext")
        alt["kltok"] = [P.sb([128, heads * 128], BF16, "kltok") for _ in range(NTT)]
        alt["vtok"] = [P.sb([128, heads * dv], BF16, "vtok") for _ in range(NTT)]
        if full:
            VH = dv // 128
            alt["qd"] = P.sb([128, heads, NB], BF16, "qd")
            alt["sT"] = [P.sb([128, heads, 128], BF16, "sT") for _ in range(NTT)]
            alt["gact"] = P.sb([128, heads * VH, NB], BF16, "gact")
        return [base, alt]

    def proj_fm(*a):
        run(proj_fm_g(*a))

    def proj_tm(*a):
        run(proj_tm_g(*a))

    def proj_fm_g(dst_fn, w, wb, col0, ntile, hT, hTb, bsel):
        for i in range(ntile):
            pb_, pbb = bank(6 + (bsel[0] % 2))
            bsel[0] += 1
            for k in range(8):
                MM(pb_[:, 0:NB], w[:, k, col0 + i * 128:col0 + (i + 1) * 128], hT[:, k, :], k == 0, k == 7, [wb, hTb], [pbb])
            dst_fn(i, pb_[:, 0:NB], pbb)
            yield

    def proj_tm_g(vtoks, w, wb, col0, width, hT, hTb, bsel):
        for tt in range(NTT):
            vt, vtb = vtoks[tt]
            for c0 in range(0, width, 512):
                cw = min(512, width - c0)
                pb_, pbb = bank(6 + (bsel[0] % 2))
                bsel[0] += 1
                for k in range(8):
                    MM(pb_[:, 0:cw], hT[:, k, tt * 128:(tt + 1) * 128], w[:, k, col0 + c0:col0 + c0 + cw], k == 0, k == 7,
                       [hTb, wb], [pbb])
                CP("act", vt[:, c0:c0 + cw], pb_[:, 0:cw], [pbb], [vtb])
                yield

    def out_proj_residual(*a, **k):
        run(out_proj_residual_g(*a, **k))

    def out_proj_residual_g(wout, woutb, ycat, ycatb, xt, xtb, l, bsel, ob0=6):
        for m in range(8):
            po, pob = bank(ob0 + (bsel[0] % 2))
            bsel[0] += 1
            for c in range(8):
                MM(po[:, 0:NB], wout[:, c, m * 128:(m + 1) * 128], ycat[:, c, :], c == 0, c == 7, [woutb, ycatb], [pob])
            STT("dve", xt[:, m, :], po[:, 0:NB], modT[:, l, 16 + m:17 + m], xt[:, m, :], ALU.mult, ALU.add,
                [pob, modb, xtb], [xtb])
            yield

    def gla_layer():
        l, heads, dv, cs = 1, 4, 256, -1.0 / 16.0
        with contextlib.ExitStack() as ls:
            P.stack = ls
            win, winb = P.sb([128, 8, 3072], BF16, "win1")
            wout, woutb = P.sb([128, 8, D], BF16, "wout1")
            wa1, wa1b = P.sb([128, 8, 16], BF16, "wa1")
            wa2, wa2b = P.sb([16, 512], F32, "wa2")
            nba, nbab = P.sb([128, 4], F32, "nba")
            ng, ngb = P.sb([128, 2], F32, "ng1")
            load_w(win, winb, IN("od_w_in"), 8)
            load_w(wout, woutb, IN("od_w_out"), 8)
            load_w(wa1, wa1b, IN("od_w_a1"), 8)
            P.dma("sp", wa2[:], IN("od_w_a2"), writes=[wa2b])
            P.dma("sp", nba[:], IN("od_baT"), writes=[nbab])
            P.dma("sp", ng[:], IN("gla_ngT"), writes=[ngb])
            TS("dve", nba[:], nba[:], -1.0, None, ALU.mult, None, [nbab], [nbab])
            St = new_state(heads, dv)

            def run_pass(full):
                with contextlib.ExitStack() as bs:
                    P.stack = bs
                    Tts = alt_tiles(recur_tiles(heads, dv, full), heads, dv, full)
                    xts = [P.sb([128, 8, NB], F32, "xt") for _ in range(2)]
                    hTs = [P.sb([128, 8, NB], BF16, "hT") for _ in range(2)]
                    NT = {"sq": P.sb([128, 8, NB], BF16, "sq"), "tmp": P.sb([128, 8, NB], F32, "tmp"),
                          "rs": P.sb([128, NB], F32, "rs")} if not full else None
                    a1s, a1sb = P.sb([16, NB], F32, "a1s")
                    e1, e1b = P.sb([128, NB], F32, "e1")
                    bsel = [0]
                    bselB = [0]

                    def front(blk):
                        t0 = blk * NB
                        Tt = Tts[blk % 2]
                        xt, xtb = xts[blk % 2]
                        hT, hTb = hTs[blk % 2]
                        P.dma("sp", xt[:], XTv[:, :, t0:t0 + NB], writes=[xtb])
                        if not full:
                            norm_block(xt, xtb, hT, hTb, gsm[:, l, :], modT[:, l, 0:8], [gsmb, modb], NT, 6)
                            P.dma("sp", HTv[:, :, t0:t0 + NB], hT[:], reads=[hTb])
                        else:
                            P.dma("sp", hT[:], HTv[:, :, t0:t0 + NB], writes=[hTb])
                        yield
                        k32, k32b = Tt["k32"]
                        lgX, lgXb = Tt["lgX"]
                        yield from proj_fm_g(lambda i, p_, pb2: CP("act", k32[:, i, :], p_, [pb2], [k32b]), win, winb, 512, 4, hT, hTb, bsel)
                        pg, pgb = bank(6 + (bsel[0] % 2))
                        bsel[0] += 1
                        for k in range(8):
                            MM(pg[0:16, 0:NB], wa1[:, k, :], hT[:, k, :], k == 0, k == 7, [wa1b, hTb], [pgb])
                        CP("act", a1s[:], pg[0:16, 0:NB], [pgb], [a1sb])
                        yield
                        for hd in range(4):
                            pz, pzb = bank(6 + (bsel[0] % 2))
                            bsel[0] += 1
                            MM(pz[:, 0:NB], wa2[:, hd * 128:(hd + 1) * 128], a1s[:], True, True, [wa2b, a1sb], [pzb])
                            ACT(e1[:], pz[:, 0:NB], AF.Exp, [pzb, nbab], [e1b], scale=-1.0, bias=nba[:, hd:hd + 1])
                            ACT(lgX[:, hd, :], e1[:], AF.Ln, [e1b, cstb], [lgXb], scale=1.0, bias=cst[:, 1:2])
                            yield
                        yield from proj_tm_g(Tt["vtok"], win, winb, 1024, 1024, hT, hTb, bsel)
                        if full:
                            qact, qactb = Tt["qact"]
                            gact, gactb = Tt["gact"]
                            yield from proj_fm_g(lambda i, p_, pb2: ACT(qact[:, i, :], p_, AF.Copy, [pb2], [qactb], scale=128.0 ** -0.5),
                                                 win, winb, 0, 4, hT, hTb, bsel)
                            yield from proj_fm_g(lambda i, p_, pb2: ACT(gact[:, i, :], p_, AF.Silu, [pb2], [gactb]),
                                                 win, winb, 2048, 8, hT, hTb, bsel)
                        else:
                            red, redb = Tt["red"]
                            dsum, dsumb = St["dsum"]
                            P.op("dve", lambda e: e.tensor_reduce(out=red[:], in_=lgX[:], axis=mybir.AxisListType.X, op=ALU.add),
                                 reads=[lgXb], writes=[redb])
                            TT("dve", dsum[:], dsum[:], red[:], ALU.add, [dsumb, redb], [dsumb])
                            yield
                        yield from recur_front(Tt, heads, dv, cs, full)

                    def back(blk):
                        t0 = blk * NB
                        Tt = Tts[blk % 2]
                        xt, xtb = xts[blk % 2]
                        yield from recur_back(Tt, St, heads, dv, cs, full)
                        if full:
                            yield from post_norm_g(Tt, heads, dv, ng, ngb, nb0=4)
                            yg, ygb = Tt["yg"]
                            yield from out_proj_residual_g(wout, woutb, yg, ygb, xt, xtb, l, bselB, ob0=4)
                            P.dma("sp", XTv[:, :, t0:t0 + NB], xt[:], reads=[xtb])
                            yield

                    run(front(0))
                    for blk in range(NBLK):
                        drive([back(blk), front(blk + 1) if blk + 1 < NBLK else None])
                    phase_end()

            run_pass(False)

            def pack(exs, exsb):
                for hd in range(heads):
                    S32, S32b = St["S32"][hd]
                    CP("dve", exs[:, hd * dv:(hd + 1) * dv], S32[:], [S32b], [exsb])
                dsum, dsumb = St["dsum"]
                CP("dve", exs[:, 1024:1024 + heads], dsum[:], [dsumb], [exsb])

            exchange(1, None, pack, lambda exg, exgb: combine_states(exg, exgb, St, heads, dv, cs, 1024), EXW)
            run_pass(True)

    def load_w_cols(dst, dstb, src, kt, c0, c1, d0=0, q="pool"):
        v = src.rearrange("(k p) c -> p k c", p=128)
        for k in range(kt):
            P.dma(q, dst[:, k, d0:d0 + (c1 - c0)], v[:, k, c0:c1], writes=[dstb])

    def sincos(ang_ap, shape, c_out, s_out, Rb, Wb, tag):
        y, yb = P.sb(shape, F32, "y" + tag)
        ki, kib = P.sb(shape, mybir.dt.int32, "ki" + tag)
        kf, kfb = P.sb(shape, F32, "kf" + tag)
        for which, dst in ((0, s_out), (1, c_out)):
            TS("dve", y[:], ang_ap, 1.0 / TWO_PI, 0.25 * which, ALU.mult, ALU.add, Rb, [yb])
            CP("dve", ki[:], y[:], [yb], [kib])
            CP("dve", kf[:], ki[:], [kib], [kfb])
            TT("dve", y[:], y[:], kf[:], ALU.subtract, [yb, kfb], [yb])
            TS("dve", kf[:], y[:], 0.5, None, ALU.is_gt, None, [yb], [kfb])
            TT("dve", y[:], y[:], kf[:], ALU.subtract, [yb, kfb], [yb])
            TS("dve", kf[:], y[:], -0.5, None, ALU.is_lt, None, [yb], [kfb])
            TT("dve", y[:], y[:], kf[:], ALU.add, [yb, kfb], [yb])
            ACT(dst, y[:], AF.Sin, [yb], Wb, scale=TWO_PI)
        return y, yb

    def ev_layer():
        l, heads, dv, cs = 0, 4, 128, 1.0
        NFR = NB // FR
        with contextlib.ExitStack() as ls:
            P.stack = ls
            wout, woutb = P.sb([128, 8, D], BF16, "wout0")
            wglu, wglub = P.sb([128, 4, 512], BF16, "wglu")
            load_w(wout, woutb, IN("ev_w_out"), 8)
            load_w(wglu, wglub, IN("s5_w_glu"), 4)
            lb, lbb = P.sb([128, 4], F32, "lb")
            oml, omlb = P.sb([128, 4], F32, "oml")
            noml, nomlb = P.sb([128, 4], F32, "noml")
            ngh, nghb = P.sb([128, 1], F32, "ngh")
            dsk, dskb = P.sb([128, 4], F32, "dsk")
            bglu, bglub = P.sb([128, 4], F32, "bglu")
            P.dma("sp", ngh[:], IN("hg_ngT"), writes=[nghb])
            P.dma("sp", dsk[:], IN("s5_dT"), writes=[dskb])
            P.dma("sp", bglu[:], IN("s5_bgluT"), writes=[bglub])
            Bp = [P.sb([128, 4, 8, 64], BF16, "Bp%d" % i) for i in range(2)]
            Cp = [P.sb([128, 16, 128], BF16, "Cp%d" % i) for i in range(2)]
            cosT, cosb = P.sb([128, 16, FR + 1], F32, "cosT")
            sinT, sinb = P.sb([128, 16, FR + 1], F32, "sinT")
            rtab, rtabb = P.sb([128, 16, FR], F32, "rtab")
            cF, cFb = P.sb([128, 16], F32, "cF")
            sF, sFb = P.sb([128, 16], F32, "sF")
            Lre, Lreb = P.sb([128, 16], F32, "Lre")
            Lim, Limb = P.sb([128, 16], F32, "Lim")
            car = [P.sb([128, 2, 4], F32, "car%d" % i) for i in range(4)]
            for i in range(4):
                MEMSET("pool", car[i][0][:], 0.0, [car[i][1]])
            St = new_state(heads, dv)

            with contextlib.ExitStack() as ss:
                P.stack = ss
                lbr, lbrb = P.sb([128, 3, 4], F32, "lbr")
                P.dma("sp", lbr[:], IN("hg_lbT"), writes=[lbrb])
                ACT(lbr[:], lbr[:], AF.Exp, [lbrb], [lbrb])
                TT("dve", oml[:], lbr[:, 0, :], lbr[:, 1, :], ALU.add, [lbrb], [omlb])
                TT("dve", oml[:], oml[:], lbr[:, 2, :], ALU.add, [omlb, lbrb], [omlb])
                RECIP(oml[:], oml[:], [omlb], [omlb])
                TT("dve", lb[:], lbr[:, 0, :], oml[:], ALU.mult, [lbrb, omlb], [lbb])
                TS("dve", oml[:], lb[:], -1.0, 1.0, ALU.mult, ALU.add, [lbb], [omlb])
                TS("dve", noml[:], oml[:], -1.0, None, ALU.mult, None, [omlb], [nomlb])

                def sbt(shape, name):
                    return P.sb(shape, F32, name)
                G3 = [128, 4, 64]
                lre, lreb = sbt(G3, "lre"); lim, limb = sbt(G3, "lim")
                bre, breb = sbt(G3, "bre"); bim, bimb = sbt(G3, "bim")
                dl, dlb = sbt([128, 4], "dl")
                P.dma("sp", lre[:], IN("lamre_gh"), writes=[lreb])
                P.dma("sp", lim[:], IN("lamim_gh"), writes=[limb])
                P.dma("sp", bre[:], IN("bre_gh"), writes=[breb])
                P.dma("sp", bim[:], IN("bim_gh"), writes=[bimb])
                P.dma("sp", dl[:], IN("lstep_gh"), writes=[dlb])
                ACT(dl[:], dl[:], AF.Exp, [dlb], [dlb])
                a_, a_b = sbt(G3, "a_"); th, thb = sbt(G3, "th")
                dl3 = dl[:].unsqueeze(2).to_broadcast(G3)
                TT("dve", a_[:], lre[:], dl3, ALU.mult, [lreb, dlb], [a_b])
                TT("dve", th[:], lim[:], dl3, ALU.mult, [limb, dlb], [thb])
                r_, r_b = sbt(G3, "r_")
                ACT(r_[:], a_[:], AF.Exp, [a_b], [r_b])
                cth, cthb = sbt(G3, "cth"); sth, sthb = sbt(G3, "sth")
                sincos(th[:], G3, cth[:], sth[:], [thb], [cthb, sthb], "g")
                er, erb = sbt(G3, "er"); ei, eib_ = sbt(G3, "ei")
                TT("dve", er[:], r_[:], cth[:], ALU.mult, [r_b, cthb, sthb], [erb])
                TS("dve", er[:], er[:], -1.0, None, ALU.add, None, [erb], [erb])
                TT("dve", ei[:], r_[:], sth[:], ALU.mult, [r_b, cthb, sthb], [eib_])
                den, denb = sbt(G3, "den"); tq, tqb = sbt(G3, "tq")
                TT("dve", den[:], lre[:], lre[:], ALU.mult, [lreb], [denb])
                TT("dve", tq[:], lim[:], lim[:], ALU.mult, [limb], [tqb])
                TT("dve", den[:], den[:], tq[:], ALU.add, [denb, tqb], [denb])
                RECIP(den[:], den[:], [denb], [denb])
                cr, crb = sbt(G3, "cr"); ci, cib = sbt(G3, "ci")
                TT("dve", cr[:], er[:], lre[:], ALU.mult, [erb, lreb], [crb])
                TT("dve", tq[:], ei[:], lim[:], ALU.mult, [eib_, limb], [tqb])
                TT("dve", cr[:], cr[:], tq[:], ALU.add, [crb, tqb], [crb])
                TT("dve", cr[:], cr[:], den[:], ALU.mult, [crb, denb], [crb])
                TT("dve", ci[:], ei[:], lre[:], ALU.mult, [eib_, lreb], [cib])
                TT("dve", tq[:], er[:], lim[:], ALU.mult, [erb, limb], [tqb])
                TT("dve", ci[:], ci[:], tq[:], ALU.subtract, [cib, tqb], [cib])
                TT("dve", ci[:], ci[:], den[:], ALU.mult, [cib, denb], [cib])
                bbr, bbrb = sbt(G3, "bbr"); bbi, bbib = sbt(G3, "bbi")
                TT("dve", bbr[:], cr[:], bre[:], ALU.mult, [crb, breb], [bbrb])
                TT("dve", tq[:], ci[:], bim[:], ALU.mult, [cib, bimb], [tqb])
                TT("dve", bbr[:], bbr[:], tq[:], ALU.subtract, [bbrb, tqb], [bbrb])
                TT("dve", bbi[:], cr[:], bim[:], ALU.mult, [crb, bimb], [bbib])
                TT("dve", tq[:], ci[:], bre[:], ALU.mult, [cib, breb], [tqb])
                TT("dve", bbi[:], bbi[:], tq[:], ALU.add, [bbib, tqb], [bbib])
                eye8, eye8b = sbt([128, 8], "eye8")
                P.dma("sp", eye8[:], IN("eye8"), writes=[eye8b])
                for ct in range(4):
                    for src_, srcb, dsti in ((bbr, bbrb, 0), (bbi, bbib, 1)):
                        TT("dve", Bp[dsti][0][:, ct, :, :], src_[:, ct, :].unsqueeze(1).to_broadcast([128, 8, 64]),
                           eye8[:].unsqueeze(2).to_broadcast([128, 8, 64]), ALU.mult, [srcb, eye8b], [Bp[dsti][1]])
                P2 = [128, 16]
                lrp, lrpb = sbt(P2, "lrp"); lip, lipb = sbt(P2, "lip"); dlp, dlpb = sbt(P2, "dlp")
                P.dma("sp", lrp[:], IN("lamre_pp"), writes=[lrpb])
                P.dma("sp", lip[:], IN("lamim_pp"), writes=[lipb])
                P.dma("sp", dlp[:], IN("lstep_pp"), writes=[dlpb])
                ACT(dlp[:], dlp[:], AF.Exp, [dlpb], [dlpb])
                ap_, ap_b = sbt(P2, "ap_"); thp, thpb = sbt(P2, "thp"); rp, rpb = sbt(P2, "rp")
                TT("dve", ap_[:], lrp[:], dlp[:], ALU.mult, [lrpb, dlpb], [ap_b])
                TT("dve", thp[:], lip[:], dlp[:], ALU.mult, [lipb, dlpb], [thpb])
                ACT(rp[:], ap_[:], AF.Exp, [ap_b], [rpb])
                CP("dve", rtab[:], rp[:].unsqueeze(2).to_broadcast([128, 16, FR]), [rpb], [rtabb])
                yk, ykb = sbt(P2, "yk"); kip, kipb = P.sb(P2, mybir.dt.int32, "kip"); kfp, kfpb = sbt(P2, "kfp")
                TS("dve", yk[:], thp[:], 1.0 / TWO_PI, None, ALU.mult, None, [thpb], [ykb])
                CP("dve", kip[:], yk[:], [ykb], [kipb])
                CP("dve", kfp[:], kip[:], [kipb], [kfpb])
                TT("dve", yk[:], yk[:], kfp[:], ALU.subtract, [ykb, kfpb], [ykb])
                TS("dve", thp[:], yk[:], TWO_PI, None, ALU.mult, None, [ykb], [thpb])
                jv, jvb = sbt([128, FR + 1], "jv")
                P.dma("sp", jv[:], IN("jvec"), writes=[jvb])
                A3 = [128, 16, FR + 1]
                ang, angb = sbt(A3, "ang")
                TT("dve", ang[:], thp[:].unsqueeze(2).to_broadcast(A3), jv[:].unsqueeze(1).to_broadcast(A3), ALU.mult,
                   [thpb, jvb], [angb])
                sincos(ang[:], A3, cosT[:], sinT[:], [angb], [cosb, sinb], "p")
                CP("dve", cF[:], cosT[:, :, FR], [cosb, sinb], [cFb])
                CP("dve", sF[:], sinT[:, :, FR], [cosb, sinb], [sFb])
                c2, c2b = sbt(P2, "c2"); s2, s2b = sbt(P2, "s2"); u1, u1b = sbt(P2, "u1"); u2, u2b = sbt(P2, "u2")
                CP("dve", c2[:], cF[:], [cFb], [c2b])
                CP("dve", s2[:], sF[:], [sFb], [s2b])
                n = FR
                while n < SEG:
                    TT("dve", u1[:], c2[:], c2[:], ALU.mult, [c2b], [u1b])
                    TT("dve", u2[:], s2[:], s2[:], ALU.mult, [s2b], [u2b])
                    TT("dve", u2[:], u1[:], u2[:], ALU.subtract, [u1b, u2b], [u2b])
                    TT("dve", u1[:], c2[:], s2[:], ALU.mult, [c2b, s2b], [u1b])
                    TS("dve", s2[:], u1[:], 2.0, None, ALU.mult, None, [u1b], [s2b])
                    CP("dve", c2[:], u2[:], [u2b], [c2b])
                    n *= 2
                ACT(u1[:], ap_[:], AF.Exp, [ap_b], [u1b], scale=float(SEG))
                TT("dve", Lre[:], u1[:], c2[:], ALU.mult, [u1b, c2b], [Lreb])
                TT("dve", Lim[:], u1[:], s2[:], ALU.mult, [u1b, s2b], [Limb])
                crp, crpb = sbt([128, 16, 16], "crp"); cip, cipb = sbt([128, 16, 16], "cip")
                cm2, cm2b = sbt([128, 4, 8], "cm2")
                P.dma("sp", crp[:], IN("cre_pp"), writes=[crpb])
                P.dma("sp", cip[:], IN("cim_pp"), writes=[cipb])
                P.dma("sp", cm2[:], IN("cmask2"), writes=[cm2b])
                C4 = [128, 4, 8, 16]
                for ct in range(4):
                    psl = slice(ct * 4, ct * 4 + 4)
                    TT("dve", Cp[0][0][:, psl, :].rearrange("p q (g h) -> p q g h", h=16),
                       crp[:, psl, :].unsqueeze(2).to_broadcast(C4), cm2[:].unsqueeze(3).to_broadcast(C4), ALU.mult,
                       [crpb, cm2b], [Cp[0][1]])
                    TT("dve", Cp[1][0][:, psl, :].rearrange("p q (g h) -> p q g h", h=16),
                       cip[:, psl, :].unsqueeze(2).to_broadcast(C4), cm2[:].unsqueeze(3).to_broadcast(C4), ALU.mult,
                       [cipb, cm2b], [Cp[1][1]])
                    TS("dve", Cp[1][0][:, psl, :], Cp[1][0][:, psl, :], -1.0, None, ALU.mult, None, [Cp[1][1]], [Cp[1][1]])
                phase_end()

            if EVSTOP <= 1:
                return

            def s5_block(*a):
                run(s5_block_g(*a))

            def s5_block_g(S5T, uT, uTb, full, gi):
                t = S5T
                for f in range(NFR):
                    fsl = slice(f * FR, (f + 1) * FR)
                    for ct in range(4):
                        bR, bRb = bank((gi[0] % 2) * 2)
                        bI, bIb = bank((gi[0] % 2) * 2 + 1)
                        gi[0] += 1
                        for q in range(4):
                            MM(bR[:, q * FR:(q + 1) * FR], Bp[0][0][:, ct, 2 * q:2 * q + 2, :].rearrange("p a b -> p (a b)"),
                               uT[:, ct, fsl], True, True, [Bp[0][1], uTb], [bRb])
                        for q in range(4):
                            MM(bI[:, q * FR:(q + 1) * FR], Bp[1][0][:, ct, 2 * q:2 * q + 2, :].rearrange("p a b -> p (a b)"),
                               uT[:, ct, fsl], True, True, [Bp[1][1], uTb], [bIb])
                        bre3 = bR[:, 0:4 * FR].rearrange("p (i j) -> p i j", j=FR)
                        bim3 = bI[:, 0:4 * FR].rearrange("p (i j) -> p i j", j=FR)
                        cs3 = cosT[:, ct * 4:ct * 4 + 4, 0:FR]
                        sn3 = sinT[:, ct * 4:ct * 4 + 4, 0:FR]
                        t1, t1b = t["t1"]; t2, t2b = t["t2"]; t3, t3b = t["t3"]; t4, t4b = t["t4"]
                        dre, dreb = t["dre"]; dim, dimb = t["dim"]; wre, wreb = t["wre"]; wim, wimb = t["wim"]
                        TT("dve", t1[:], bre3, cs3, ALU.mult, [bRb, cosb], [t1b])
                        TT("dve", t2[:], bim3, sn3, ALU.mult, [bIb, sinb], [t2b])
                        TT("pool", dre[:], t1[:], t2[:], ALU.add, [t1b, t2b], [dreb])
                        yield
                        TT("dve", t3[:], bim3, cs3, ALU.mult, [bIb, cosb], [t3b])
                        TT("dve", t4[:], bre3, sn3, ALU.mult, [bRb, sinb], [t4b])
                        TT("pool", dim[:], t3[:], t4[:], ALU.subtract, [t3b, t4b], [dimb])
                        yield
                        cr_, crb_ = car[ct]
                        for q in range(4):
                            pi_ = ct * 4 + q
                            SCAN(wre[:, q, :], rtab[:, pi_, :], dre[:, q, :], cr_[:, 0, q:q + 1], [rtabb, dreb, crb_], [wreb])
                            SCAN(wim[:, q, :], rtab[:, pi_, :], dim[:, q, :], cr_[:, 1, q:q + 1], [rtabb, dimb, crb_], [wimb])
                            yield
                        k1, k1b = t["k1"]; k2, k2b = t["k2"]
                        cFh = cF[:, ct * 4:ct * 4 + 4]
                        sFh = sF[:, ct * 4:ct * 4 + 4]
                        wlr = wre[:, :, FR - 1]
                        wli = wim[:, :, FR - 1]
                        TT("dve", k1[:], wlr, cFh, ALU.mult, [wreb, cFb], [k1b])
                        TT("dve", k2[:], wli, sFh, ALU.mult, [wimb, sFb], [k2b])
                        TT("dve", cr_[:, 0, :], k1[:], k2[:], ALU.subtract, [k1b, k2b], [crb_])
                        TT("dve", k1[:], wlr, sFh, ALU.mult, [wreb, sFb], [k1b])
                        TT("dve", k2[:], wli, cFh, ALU.mult, [wimb, cFb], [k2b])
                        TT("dve", cr_[:, 1, :], k1[:], k2[:], ALU.add, [k1b, k2b], [crb_])
                        yield
                        if full:
                            xre, xreb = t["xre"]; xim, ximb = t["xim"]
                            TT("dve", t1[:], wre[:], cs3, ALU.mult, [wreb, cosb], [t1b])
                            TT("dve", t2[:], wim[:], sn3, ALU.mult, [wimb, sinb], [t2b])
                            TT("dve", xre[:], t1[:], t2[:], ALU.subtract, [t1b, t2b], [xreb])
                            TT("dve", t3[:], wre[:], sn3, ALU.mult, [wreb, sinb], [t3b])
                            TT("dve", t4[:], wim[:], cs3, ALU.mult, [wimb, cosb], [t4b])
                            TT("dve", xim[:], t3[:], t4[:], ALU.add, [t3b, t4b], [ximb])
                            bY, bYb = bank(4 + ct // 2)
                            reg = bY[:, (ct % 2) * NB + f * FR:(ct % 2) * NB + (f + 1) * FR]
                            for q in range(4):
                                pi_ = ct * 4 + q
                                MM(reg, Cp[0][0][:, pi_, :], xre[:, q, :], q == 0, False, [Cp[0][1], xreb], [bYb])
                                MM(reg, Cp[1][0][:, pi_, :], xim[:, q, :], False, q == 3, [Cp[1][1], ximb], [bYb])

            def s5_tiles(full):
                t = {k: P.sb([128, 4, FR], F32, k) for k in ("t1", "t2", "t3", "t4", "dre", "dim", "wre", "wim")}
                t["k1"] = P.sb([128, 4], F32, "k1")
                t["k2"] = P.sb([128, 4], F32, "k2")
                if full:
                    t["xre"] = P.sb([128, 4, FR], BF16, "xre")
                    t["xim"] = P.sb([128, 4, FR], BF16, "xim")
                return t

            def hg_kpath(Tt, win, winb, cF_, cI_, hT, hTb, bsel, sg, sgb):
                k32, k32b = Tt["k32"]
                lgX, lgXb = Tt["lgX"]
                proj_fm(lambda i, p_, pb2: ACT(sg[:, i, :], p_, AF.Sigmoid, [pb2], [sgb]), win, winb, cF_, 4, hT, hTb, bsel)
                for hd in range(4):
                    ACT(lgX[:, hd, :], sg[:, hd, :], AF.Ln, [sgb, omlb, lbb], [lgXb], scale=oml[:, hd:hd + 1], bias=lb[:, hd:hd + 1])
                    TS("dve", k32[:, hd, :], sg[:, hd, :], noml[:, hd:hd + 1], oml[:, hd:hd + 1], ALU.mult, ALU.add,
                       [sgb, nomlb, omlb], [k32b])
                proj_tm(Tt["vtok"], win, winb, cI_, 512, hT, hTb, bsel)

            with contextlib.ExitStack() as bs:
                P.stack = bs
                win, winb = P.sb([128, 8, 1536], BF16, "winA")
                load_w_cols(win, winb, IN("ev_w_in"), 8, 0, 512, 0)
                load_w_cols(win, winb, IN("ev_w_in"), 8, 1024, 2048, 512)
                Tt = recur_tiles(heads, dv, False)
                S5T = s5_tiles(False)
                xts = [P.sb([128, 8, NB], F32, "xt") for _ in range(2)]
                hTs = [P.sb([128, 8, NB], BF16, "hT") for _ in range(2)]
                NT = {"sq": P.sb([128, 8, NB], BF16, "sq"), "tmp": P.sb([128, 8, NB], F32, "tmp"),
                      "rs": P.sb([128, NB], F32, "rs")}
                sg, sgb = P.sb([128, 4, NB], F32, "sg")
                uT, uTb = P.sb([128, 4, NB], BF16, "uT")
                bsel = [0]
                gi = [0]
                for blk in range(NBLK):
                    t0 = blk * NB
                    xt, xtb = xts[blk % 2]
                    hT, hTb = hTs[blk % 2]
                    P.dma("sp", xt[:], XTv[:, :, t0:t0 + NB], writes=[xtb])
                    norm_block(xt, xtb, hT, hTb, gsm[:, l, :], modT[:, l, 0:8], [gsmb, modb], NT, 6)
                    P.dma("sp", HTv[:, :, t0:t0 + NB], hT[:], reads=[hTb])
                    proj_fm(lambda i, p_, pb2: CP("act", uT[:, i, :], p_, [pb2], [uTb]), win, winb, 0, 4, hT, hTb, bsel)
                    def hg_stream():
                        hg_kpath(Tt, win, winb, 512, 1024, hT, hTb, bsel, sg, sgb)
                        yield
                        yield from recur_front(Tt, heads, dv, cs, False, trb=6)
                        yield from recur_back(Tt, St, heads, dv, cs, False)
                    drive([s5_block_g(S5T, uT, uTb, False, gi), hg_stream()])
                    red, redb = Tt["red"]
                    lgX, lgXb = Tt["lgX"]
                    dsum, dsumb = St["dsum"]
                    P.op("dve", lambda e: e.tensor_reduce(out=red[:], in_=lgX[:], axis=mybir.AxisListType.X, op=ALU.add),
                         reads=[lgXb], writes=[redb])
                    TT("dve", dsum[:], dsum[:], red[:], ALU.add, [dsumb, redb], [dsumb])
                phase_end()

            if EVSTOP <= 2:
                return

            def pack(exs, exsb):
                for hd in range(heads):
                    S32, S32b = St["S32"][hd]
                    CP("dve", exs[:, hd * dv:(hd + 1) * dv], S32[:], [S32b], [exsb])
                dsum, dsumb = St["dsum"]
                CP("dve", exs[:, 1024:1024 + heads], dsum[:], [dsumb], [exsb])
                for ct in range(4):
                    CP("dve", exs[:, 1028 + ct * 8:1028 + ct * 8 + 8].rearrange("p (a b) -> p a b", b=4), car[ct][0][:],
                       [car[ct][1]], [exsb])

            def unpack(exg, exgb):
                combine_states(exg, exgb, St, heads, dv, cs, 1024)
                ar, arb = P.sb([128, 16], F32, "ar"); ai, aib = P.sb([128, 16], F32, "ai")
                nr, nrb = P.sb([128, 16], F32, "nr"); ni, nib = P.sb([128, 16], F32, "ni")
                v1, v1b = P.sb([128, 16], F32, "v1")
                MEMSET("pool", ar[:], 0.0, [arb])
                MEMSET("pool", ai[:], 0.0, [aib])
                for r in range(3):
                    cl = exg[:, r, 1028:1060].rearrange("p (c a b) -> p c a b", a=2, b=4)
                    clr = cl[:, :, 0, :]
                    cli = cl[:, :, 1, :]
                    ar4 = ar[:].rearrange("p (c b) -> p c b", b=4); ai4 = ai[:].rearrange("p (c b) -> p c b", b=4)
                    nr4 = nr[:].rearrange("p (c b) -> p c b", b=4); ni4 = ni[:].rearrange("p (c b) -> p c b", b=4)
                    TT("dve", nr[:], Lre[:], ar[:], ALU.mult, [Lreb, arb], [nrb])
                    TT("dve", v1[:], Lim[:], ai[:], ALU.mult, [Limb, aib], [v1b])
                    TT("dve", nr[:], nr[:], v1[:], ALU.subtract, [nrb, v1b], [nrb])
                    TT("dve", nr4, nr4, clr, ALU.add, [nrb, exgb], [nrb])
                    TT("dve", ni[:], Lre[:], ai[:], ALU.mult, [Lreb, aib], [nib])
                    TT("dve", v1[:], Lim[:], ar[:], ALU.mult, [Limb, arb], [v1b])
                    TT("dve", ni[:], ni[:], v1[:], ALU.add, [nib, v1b], [nib])
                    TT("dve", ni4, ni4, cli, ALU.add, [nib, exgb], [nib])
                    TT("dve", nr[:], nr[:], ar[:], ALU.subtract, [nrb, arb], [nrb])
                    TT("dve", ni[:], ni[:], ai[:], ALU.subtract, [nib, aib], [nib])
                    STT("dve", ar[:], nr[:], segm[:, r:r + 1], ar[:], ALU.mult, ALU.add, [nrb, segmb, arb], [arb])
                    STT("dve", ai[:], ni[:], segm[:, r:r + 1], ai[:], ALU.mult, ALU.add, [nib, segmb, aib], [aib])
                for ct in range(4):
                    CP("dve", car[ct][0][:, 0, :], ar[:, ct * 4:ct * 4 + 4], [arb], [car[ct][1]])
                    CP("dve", car[ct][0][:, 1, :], ai[:, ct * 4:ct * 4 + 4], [aib], [car[ct][1]])

            exchange(0, None, pack, unpack, EXW)
            if EVSTOP <= 3:
                return

            with contextlib.ExitStack() as bs:
                P.stack = bs
                win, winb = P.sb([128, 8, 512], BF16, "winU")
                load_w_cols(win, winb, IN("ev_w_in"), 8, 0, 512, 0)
                S5T = s5_tiles(True)
                hTs = [P.sb([128, 8, NB], BF16, "hT") for _ in range(2)]
                uT, uTb = P.sb([128, 4, NB], BF16, "uT")
                u32s = [P.sb([128, 4, NB], F32, "u32") for _ in range(2)]
                uTs = [P.sb([128, 4, NB], BF16, "uTb") for _ in range(2)]
                y32, y32b = P.sb([128, 4, NB], F32, "y32")
                g1, g1b = P.sb([128, 4, NB], F32, "g1")
                g2, g2b = P.sb([128, 4, NB], F32, "g2")
                gl, glb = P.sb([128, 4, NB], F32, "gl")
                glh, glhb = P.sb([128, 4, NB], BF16, "glh")
                sg2s = [P.sb([128, NB], F32, "sg2") for _ in range(2)]
                yas = [P.sb([128, 4, NB], BF16, "ya") for _ in range(2)]
                bsel = [0]
                bselB = [0]
                gi = [0]

                def front(blk):
                    t0 = blk * NB
                    hT, hTb = hTs[blk % 2]
                    u32, u32b = u32s[blk % 2]
                    uT_, uT_b = uTs[blk % 2]
                    P.dma("sp", hT[:], HTv[:, :, t0:t0 + NB], writes=[hTb])

                    def put_u(i, p_, pb2):
                        CP("act", u32[:, i, :], p_, [pb2], [u32b])
                        CP("dve", uT_[:, i, :], u32[:, i, :], [u32b], [uT_b])
                    yield from proj_fm_g(put_u, win, winb, 0, 4, hT, hTb, bsel)
                    yield from s5_block_g(S5T, uT_, uT_b, True, gi)

                def back(blk):
                    t0 = blk * NB
                    u32, u32b = u32s[blk % 2]
                    ya, yab = yas[blk % 2]
                    for ct in range(4):
                        bY, bYb = bank(4 + ct // 2)
                        STT("dve", y32[:, ct, :], u32[:, ct, :], dsk[:, ct:ct + 1], bY[:, (ct % 2) * NB:(ct % 2) * NB + NB],
                            ALU.mult, ALU.add, [u32b, dskb, bYb], [y32b])
                    yield
                    TT("pool", g1[:], y32[:], y32[:], ALU.mult, [y32b], [g1b])
                    yield
                    TS("dve", g1[:], g1[:], 0.044715, 1.0, ALU.mult, ALU.add, [g1b], [g1b])
                    yield
                    TT("pool", g2[:], g1[:], y32[:], ALU.mult, [g1b, y32b], [g2b])
                    yield
                    ACT(g1[:], g2[:], AF.Sigmoid, [g2b], [g1b], scale=1.5957691216057308)
                    yield
                    TT("dve", gl[:], y32[:], g1[:], ALU.mult, [y32b, g1b], [glb])
                    CP("act", glh[:], gl[:], [glb], [glhb])
                    yield
                    for co in range(4):
                        pz, pzb = bank(6 + (bselB[0] % 2))
                        bselB[0] += 1
                        for ct in range(4):
                            MM(pz[:, 0:NB], wglu[:, ct, co * 128:(co + 1) * 128], glh[:, ct, :], ct == 0, ct == 3,
                               [wglub, glhb], [pzb])
                        s2_, s2b_ = sg2s[co % 2]
                        ACT(s2_[:], pz[:, 0:NB], AF.Sigmoid, [pzb, bglub], [s2b_], bias=bglu[:, co:co + 1])
                        TT("dve", ya[:, co, :], gl[:, co, :], s2_[:], ALU.mult, [glb, s2b_], [yab])
                        yield
                    P.dma("sp", YAv[:, :, t0:t0 + NB], ya[:], reads=[yab])
                    yield

                run(front(0))
                for blk in range(NBLK):
                    drive([back(blk), front(blk + 1) if blk + 1 < NBLK else None])
                phase_end()

            if EVSTOP <= 4:
                return

            with contextlib.ExitStack() as bs:
                P.stack = bs
                win, winb = P.sb([128, 8, 2048], BF16, "winB")
                load_w_cols(win, winb, IN("ev_w_in"), 8, 512, 2560, 0)
                Tts = alt_tiles(recur_tiles(heads, dv, True), heads, dv, True)
                xts = [P.sb([128, 8, NB], F32, "xt") for _ in range(2)]
                hTs = [P.sb([128, 8, NB], BF16, "hT") for _ in range(2)]
                ycs = [P.sb([128, 8, NB], BF16, "ycat") for _ in range(2)]
                sg, sgb = P.sb([128, 4, NB], F32, "sg")
                bsel = [0]
                bselB = [0]

                def front(blk):
                    t0 = blk * NB
                    Tt = Tts[blk % 2]
                    xt, xtb = xts[blk % 2]
                    hT, hTb = hTs[blk % 2]
                    yc, ycb = ycs[blk % 2]
                    P.dma("sp", xt[:], XTv[:, :, t0:t0 + NB], writes=[xtb])
                    P.dma("sp", hT[:], HTv[:, :, t0:t0 + NB], writes=[hTb])
                    P.dma("sp", yc[:, 0:4, :], YAv[:, :, t0:t0 + NB], writes=[ycb])
                    yield
                    k32, k32b = Tt["k32"]
                    lgX, lgXb = Tt["lgX"]
                    yield from proj_fm_g(lambda i, p_, pb2: ACT(sg[:, i, :], p_, AF.Sigmoid, [pb2], [sgb]), win, winb, 512, 4, hT, hTb, bsel)
                    for hd in range(4):
                        ACT(lgX[:, hd, :], sg[:, hd, :], AF.Ln, [sgb, omlb, lbb], [lgXb], scale=oml[:, hd:hd + 1], bias=lb[:, hd:hd + 1])
                        TS("dve", k32[:, hd, :], sg[:, hd, :], noml[:, hd:hd + 1], oml[:, hd:hd + 1], ALU.mult, ALU.add,
                           [sgb, nomlb, omlb], [k32b])
                        yield
                    yield from proj_tm_g(Tt["vtok"], win, winb, 1024, 512, hT, hTb, bsel)
                    qact, qactb = Tt["qact"]
                    gact, gactb = Tt["gact"]
                    yield from proj_fm_g(lambda i, p_, pb2: ACT(qact[:, i, :], p_, AF.Silu, [pb2], [qactb]), win, winb, 0, 4, hT, hTb, bsel)
                    yield from proj_fm_g(lambda i, p_, pb2: ACT(gact[:, i, :], p_, AF.Silu, [pb2], [gactb]), win, winb, 1536, 4, hT, hTb, bsel)
                    yield from recur_front(Tt, heads, dv, cs, True)

                def back(blk):
                    t0 = blk * NB
                    Tt = dict(Tts[blk % 2])
                    xt, xtb = xts[blk % 2]
                    yc, ycb = ycs[blk % 2]
                    yield from recur_back(Tt, St, heads, dv, cs, True)
                    Tt["yg_ap"] = (yc[:, 4:8, :], ycb)
                    yield from post_norm_g(Tt, heads, dv, ngh, nghb, nb0=4)
                    yield from out_proj_residual_g(wout, woutb, yc, ycb, xt, xtb, l, bselB, ob0=4)
                    P.dma("sp", XTv[:, :, t0:t0 + NB], xt[:], reads=[xtb])
                    yield

                run(front(0))
                for blk in range(NBLK):
                    drive([back(blk), front(blk + 1) if blk + 1 < NBLK else None])
                phase_end()

    for st in stages:
        if st == "F0":
            ffn_phase(0)
        elif st == "F1":
            ffn_phase(1)
        elif st == "L1":
            gla_layer()
        elif st == "L0":
            ev_layer()
    out_phase()
    main.close()
    nc._used_inputs = set(_ins.keys())
    return nc


def _colT(v, nt):
    return np.ascontiguousarray(np.asarray(v, np.float32).reshape(nt, 128).T)


def host_inputs(inp, SEG, core):
    b, s = core // 4, core % 4
    f32 = np.float32
    g = lambda k: np.asarray(inp[k], f32)
    m = {}
    m["x"] = np.ascontiguousarray(g("x")[b, s * SEG:(s + 1) * SEG, :])
    m["cT"] = _colT(g("c")[b], 8)
    m["ada_w"] = g("ada_w")
    m["ada_bT"] = np.ascontiguousarray(g("ada_b").reshape(2, 48, 128).transpose(2, 0, 1))
    m["nmgT"] = np.ascontiguousarray(g("norm_mix_g").reshape(2, 8, 128).transpose(2, 0, 1))
    m["nfgT"] = np.ascontiguousarray(g("norm_ffn_g").reshape(2, 8, 128).transpose(2, 0, 1))
    m["fngT"] = _colT(g("final_norm_g"), 8)
    m["ev_w_in"] = g("ev_w_in")[0]
    m["ev_w_out"] = g("ev_w_out")[0]
    G, Pn, H = 32, 64, 16
    lre, lim, lst = g("s5_lam_re")[0], g("s5_lam_im")[0], g("s5_log_step")[0]
    rep = lambda a: np.ascontiguousarray(np.repeat(a, H, axis=0).reshape(4, 128, -1).transpose(1, 0, 2))
    m["lamre_gh"] = rep(lre)
    m["lamim_gh"] = rep(lim)
    m["lstep_gh"] = np.ascontiguousarray(rep(lst[:, None])[:, :, 0])
    tb = lambda a: np.ascontiguousarray(a.transpose(0, 2, 1).reshape(4, 128, Pn).transpose(1, 0, 2))
    m["bre_gh"] = tb(g("s5_b_re")[0])
    m["bim_gh"] = tb(g("s5_b_im")[0])
    pp = lambda a: np.ascontiguousarray(a.reshape(16, 2, Pn).transpose(1, 2, 0).reshape(128, 16))
    m["lamre_pp"] = pp(lre)
    m["lamim_pp"] = pp(lim)
    m["lstep_pp"] = pp(np.repeat(lst[:, None], Pn, axis=1))
    cp = lambda a: np.ascontiguousarray(a.reshape(16, 2, H, Pn).transpose(1, 3, 0, 2).reshape(128, 16, H))
    m["cre_pp"] = cp(g("s5_c_re")[0])
    m["cim_pp"] = cp(g("s5_c_im")[0])
    m["s5_dT"] = _colT(g("s5_d")[0], 4)
    m["s5_w_glu"] = g("s5_w_glu")[0]
    m["s5_bgluT"] = _colT(g("s5_b_glu")[0], 4)
    m["hg_lbT"] = np.ascontiguousarray(g("hg_lb_logits").reshape(3, 4, 128).transpose(2, 0, 1))
    m["hg_ngT"] = np.ascontiguousarray(g("hg_norm_g")[0].reshape(128, 1))
    m["od_w_in"] = g("od_w_in")[0]
    m["od_w_a1"] = g("od_w_a1")[0]
    m["od_w_a2"] = g("od_w_a2")[0]
    m["od_baT"] = _colT(g("od_b_a")[0], 4)
    m["gla_ngT"] = _colT(g("gla_norm_g")[0], 2)
    m["od_w_out"] = g("od_w_out")[0]
    m["ffn_w1"] = g("ffn_w1")
    m["ffn_w3"] = g("ffn_w3")
    m["ffn_w2"] = g("ffn_w2")
    m["ident"] = np.eye(128, dtype=f32)
    j = np.arange(128)
    m["smask"] = ((j[:, None] // 64 == j[None, :] // 64) & (j[None, :] >= j[:, None])).astype(f32)
    rm = np.ones((128, NB), f32)
    rm[:, ::64] = 0.0
    m["rmask"] = rm
    m["jvec"] = np.ascontiguousarray(np.broadcast_to(np.arange(FR + 1, dtype=f32), (128, FR + 1)))
    m["eye8"] = (j[:, None] // 16 == np.arange(8)[None, :]).astype(f32)
    g2 = j // 64
    m["cmask2"] = (np.arange(8)[None, None, :] == (2 * np.arange(4)[None, :, None] + g2[:, None, None])).astype(f32)
    m["segm"] = np.ascontiguousarray(np.broadcast_to((np.arange(3) < s).astype(f32), (128, 3)))
    return m


_NC_CACHE = {}


def kernel(**inp):
    SEG = np.asarray(inp["x"]).shape[1] // 4
    key = SEG
    if key not in _NC_CACHE:
        _NC_CACHE[key] = build(SEG)
    nc = _NC_CACHE[key]
    in_maps = [{k: v for k, v in host_inputs(inp, SEG, c).items() if k in nc._used_inputs} for c in range(8)]
    res = run_bass_kernel_spmd(nc, in_maps, core_ids=list(range(8)))
    o = np.empty((2, 4 * SEG, D), np.float32)
    for c in range(8):
        b, s = c // 4, c % 4
        o[b, s * SEG:(s + 1) * SEG, :] = res.results[c]["out"]
    return o
```

```python
import contextlib
import numpy as np
import concourse.bass as bass
import concourse.mybir as mybir
from concourse.bass_utils import run_bass_kernel_spmd

F32 = mybir.dt.float32
BF16 = mybir.dt.bfloat16
ALU = mybir.AluOpType
AF = mybir.ActivationFunctionType


class Buf:
    __slots__ = ("name", "w", "r")

    def __init__(self, name):
        self.name = name
        self.w = None
        self.r = {}


class Prog:
    ENG = ("pe", "dve", "act", "pool", "sp")
    NSLOT = 6

    def __init__(self, nc, stack):
        self.nc = nc
        self.stack = stack
        self.h = {"pe": nc.tensor, "dve": nc.vector, "act": nc.scalar,
                  "pool": nc.gpsimd, "sp": nc.sync}
        self.sem = {e: stack.enter_context(nc.semaphore("s_" + e)) for e in self.ENG}
        self.cnt = {e: 0 for e in self.ENG}
        self.ops = {e: [] for e in self.ENG}
        self.known = {e: {} for e in self.ENG}
        self.semid = {}
        for e in self.ENG:
            self.semid[id(self.sem[e])] = self.sem[e]
        self.dsem = {}
        self.dcnt = {}
        self.dnext = {}
        for q in ("sp", "pool", "act"):
            self.dsem[q] = [stack.enter_context(nc.semaphore("d_%s%d" % (q, i)))
                            for i in range(self.NSLOT)]
            self.dcnt[q] = [0] * self.NSLOT
            self.dnext[q] = 0
        self.bufs = []
        self.ntile = 0

    def buf(self, name):
        b = Buf(name)
        self.bufs.append(b)
        return b

    def sb(self, shape, dt, name=None):
        self.ntile += 1
        name = "%s_%d" % (name or "t", self.ntile)
        t = self.stack.enter_context(self.nc.sbuf_tensor(name, list(shape), dt))
        t_b = self.buf(name)
        return t, t_b

    def ps(self, shape, dt=F32, name=None):
        self.ntile += 1
        name = "%s_%d" % (name or "p", self.ntile)
        t = self.stack.enter_context(self.nc.psum_tensor(name, list(shape), dt))
        return t, self.buf(name)

    def _deps(self, reads, writes, own=None):
        toks = []
        for b in reads:
            if b.w is not None:
                toks.append(b.w)
        for b in writes:
            if b.w is not None and b.w[0] is not own:
                toks.append(b.w)
            toks.extend(t for t in b.r.values() if t[0] is not own)
        return toks

    def _waits(self, eng, toks):
        best = {}
        for (s, v) in toks:
            k = id(s)
            if v > best.get(k, (None, 0))[1]:
                best[k] = (s, v)
        out = []
        kn = self.known[eng]
        for k, (s, v) in best.items():
            if kn.get(k, 0) >= v:
                continue
            kn[k] = v
            out.append((s, v))
        return out

    def _commit(self, tok, reads, writes):
        for b in writes:
            b.w = tok
            b.r = {}
        for b in reads:
            b.r[id(tok[0])] = tok

    def op(self, eng, fn, reads=(), writes=()):
        toks = self._deps(reads, writes, own=self.sem[eng])
        if eng == "pe":
            toks = [t for t in toks if t[0] is not self.sem["pe"]]
        waits = self._waits(eng, toks)
        self.cnt[eng] += 1
        tok = (self.sem[eng], self.cnt[eng])
        self.ops[eng].append((waits, fn, self.sem[eng], 1))
        self._commit(tok, reads, writes)
        return tok

    def dma(self, q, out, in_, reads=(), writes=(), **kw):
        toks = self._deps(reads, writes)
        i = self.dnext[q]
        self.dnext[q] = (i + 1) % self.NSLOT
        s = self.dsem[q][i]
        if self.dcnt[q][i] > 0:
            toks.append((s, 16 * self.dcnt[q][i]))
        waits = self._waits(q, toks)
        self.dcnt[q][i] += 1
        tok = (s, 16 * self.dcnt[q][i])
        self.ops[q].append((waits, lambda e: e.dma_start(out=out, in_=in_, **kw), s, 16))
        self._commit(tok, reads, writes)
        return tok

    def raw(self, eng, fn, sem, inc, reads=(), writes=(), extra=()):
        toks = self._deps(reads, writes) + list(extra)
        waits = self._waits(eng, toks)
        self.ops[eng].append((waits, fn, sem, inc))

    def barrier(self):
        toks = [(self.sem[e], self.cnt[e]) for e in self.ENG if self.cnt[e] > 0]
        for q in self.dsem:
            for i in range(self.NSLOT):
                if self.dcnt[q][i] > 0:
                    toks.append((self.dsem[q][i], 16 * self.dcnt[q][i]))
        for e in self.ENG:
            waits = self._waits(e, toks)
            if waits:
                self.ops[e].append((waits, None, None, 0))
        for b in self.bufs:
            b.w = None
            b.r = {}
        self.bufs = []

    def emit(self):
        nc = self.nc
        with nc.Block() as block:
            def run(e, lst):
                for (waits, fn, sem, inc) in lst:
                    for (s, v) in waits:
                        e.wait_ge(s, v)
                    if fn is not None:
                        ins = fn(e)
                        if sem is not None:
                            ins.then_inc(sem, inc)

            @block.tensor
            def _(e):
                run(e, self.ops["pe"])

            @block.vector
            def _(e):
                run(e, self.ops["dve"])

            @block.scalar
            def _(e):
                run(e, self.ops["act"])

            @block.gpsimd
            def _(e):
                run(e, self.ops["pool"])

            @block.sync
            def _(e):
                run(e, self.ops["sp"])
        self.ops = {e: [] for e in self.ENG}


D = 1024
KC = 8
FH = 2816
JH = 22
EPS = 1e-6
NB = 256
FR = 128
TWO_PI = float(2 * np.pi)


def build(SEG=4096, stages=("L0", "F0", "L1", "F1"), dbg=False, QB="pool", GROUPS=((0, 1, 2, 3), (4, 5, 6, 7)), EVSTOP=9):
    GROUPS = [list(g) for g in GROUPS]
    nc = bass.Bass("TRN2", target_bir_lowering=False)
    NBLK = SEG // NB
    NTT = NB // 128
    NCH = NB // 64

    INSHAPE = {
        "x": [SEG, D],
        "cT": [128, 8],
        "ada_w": [2, D, 6 * D],
        "ada_bT": [128, 2, 48],
        "nmgT": [128, 2, 8],
        "nfgT": [128, 2, 8],
        "fngT": [128, 8],
        "ev_w_in": [D, 2560],
        "ev_w_out": [D, D],
        "lamre_gh": [128, 4, 64],
        "lamim_gh": [128, 4, 64],
        "lstep_gh": [128, 4],
        "bre_gh": [128, 4, 64],
        "bim_gh": [128, 4, 64],
        "lamre_pp": [128, 16],
        "lamim_pp": [128, 16],
        "lstep_pp": [128, 16],
        "cre_pp": [128, 16, 16],
        "cim_pp": [128, 16, 16],
        "s5_dT": [128, 4],
        "s5_w_glu": [512, 512],
        "s5_bgluT": [128, 4],
        "hg_lbT": [128, 3, 4],
        "hg_ngT": [128, 1],
        "od_w_in": [D, 3072],
        "od_w_a1": [D, 16],
        "od_w_a2": [16, 512],
        "od_baT": [128, 4],
        "gla_ngT": [128, 2],
        "od_w_out": [D, D],
        "ffn_w1": [2, D, FH],
        "ffn_w3": [2, D, FH],
        "ffn_w2": [2, FH, D],
        "ident": [128, 128],
        "smask": [128, 128],
        "rmask": [128, NB],
        "jvec": [128, FR + 1],
        "eye8": [128, 8],
        "cmask2": [128, 4, 8],
        "segm": [128, 3],
    }
    _ins = {}

    def IN(name):
        if name not in _ins:
            _ins[name] = nc.dram_tensor(name, list(INSHAPE[name]), F32, kind="ExternalInput").ap()
        return _ins[name]

    out = nc.dram_tensor("out", [SEG, D], F32, kind="ExternalOutput").ap()
    XT = nc.dram_tensor("XT", [8, 128, SEG], F32).ap()
    HT = nc.dram_tensor("HT", [8, 128, SEG], BF16).ap()
    YA = nc.dram_tensor("YA", [4, 128, SEG], BF16).ap()
    YAv = YA.rearrange("k p t -> p k t")
    XTv = XT.rearrange("k p t -> p k t")
    HTv = HT.rearrange("k p t -> p k t")
    EXW = 1028 + 64
    ex_in = nc.dram_tensor("ex_in", [128, EXW], F32)
    ex_out = nc.dram_tensor("ex_out", [4 * 128, EXW], F32)
    ex_in2 = nc.dram_tensor("ex_in2", [128, EXW], F32)
    ex_out2 = nc.dram_tensor("ex_out2", [4 * 128, EXW], F32)

    main = contextlib.ExitStack()
    main.__enter__()
    P = Prog(nc, main)
    cc_sem = main.enter_context(nc.semaphore("cc"))
    cc_count = [0]

    def MM(o, lhsT, rhs, start, stop, R, W):
        P.op("pe", lambda e: e.matmul(o, lhsT=lhsT, rhs=rhs, start=start, stop=stop), reads=R, writes=W)

    def ACT(o, i, func, R, W, scale=1.0, bias=None):
        if bias is None:
            P.op("act", lambda e: e.activation(out=o, in_=i, func=func, scale=scale), reads=R, writes=W)
        else:
            P.op("act", lambda e: e.activation(out=o, in_=i, func=func, scale=scale, bias=bias), reads=R, writes=W)

    def TT(eng, o, a, b, op, R, W):
        P.op(eng, lambda e: e.tensor_tensor(out=o, in0=a, in1=b, op=op), reads=R, writes=W)

    def TS(eng, o, a, s1, s2, op0, op1, R, W):
        if s2 is None:
            P.op(eng, lambda e: e.tensor_scalar(out=o, in0=a, scalar1=s1, scalar2=None, op0=op0), reads=R, writes=W)
        else:
            P.op(eng, lambda e: e.tensor_scalar(out=o, in0=a, scalar1=s1, scalar2=s2, op0=op0, op1=op1), reads=R, writes=W)

    def STT(eng, o, a, s, b, op0, op1, R, W):
        P.op(eng, lambda e: e.scalar_tensor_tensor(out=o, in0=a, scalar=s, in1=b, op0=op0, op1=op1), reads=R, writes=W)

    def CP(eng, o, i, R, W):
        if eng == "act":
            P.op("act", lambda e: e.activation(out=o, in_=i, func=AF.Copy), reads=R, writes=W)
        else:
            P.op(eng, lambda e: e.tensor_copy(out=o, in_=i), reads=R, writes=W)

    def SCAN(o, d0, d1, init, R, W):
        P.op("dve", lambda e: e.tensor_tensor_scan(out=o, data0=d0, data1=d1, initial=init,
                                                   op0=ALU.mult, op1=ALU.add), reads=R, writes=W)

    def MEMSET(eng, o, v, W):
        P.op(eng, lambda e: e.memset(o, v), writes=W)

    def RECIP(o, i, R, W):
        P.op("dve", lambda e: e.reciprocal(out=o, in_=i), reads=R, writes=W)

    ident, identb = P.sb([128, 128], F32, "ident")
    ones_bf, onesb = P.sb([128, 128], BF16, "ones")
    smask, smaskb = P.sb([128, 128], F32, "smask")
    rmask, rmaskb = P.sb([128, NB], F32, "rmask")
    cst, cstb = P.sb([128, 8], F32, "cst")
    modT, modb = P.sb([128, 2, 48], F32, "mod")
    gsm, gsmb = P.sb([128, 2, 8], F32, "gsm")
    gsf, gsfb = P.sb([128, 2, 8], F32, "gsf")
    fng, fngb = P.sb([128, 8], F32, "fng")
    segm, segmb = P.sb([128, 3], F32, "segm")

    PB = []
    PBb = []
    for i in range(4):
        t = main.enter_context(nc.psum_tensor("pb%d" % i, [128, 1024], F32))
        PB.append(t)
        PBb.append(P.buf("pbA%d" % i))
        PBb.append(P.buf("pbB%d" % i))

    def bank(i):
        return PB[i // 2][:, (i % 2) * 512:(i % 2) * 512 + 512], PBb[i]

    def phase_end():
        P.barrier()
        P.emit()

    with contextlib.ExitStack() as ps:
        P.stack = ps
        P.dma("sp", ident[:], IN("ident"), writes=[identb])
        P.dma("sp", smask[:], IN("smask"), writes=[smaskb])
        P.dma("sp", rmask[:], IN("rmask"), writes=[rmaskb])
        P.dma("sp", fng[:], IN("fngT"), writes=[fngb])
        P.dma("sp", segm[:], IN("segm"), writes=[segmb])
        MEMSET("pool", ones_bf[:], 1.0, [onesb])
        MEMSET("pool", cst[:, 0:1], EPS, [cstb])
        MEMSET("pool", cst[:, 1:2], 1.0, [cstb])
        MEMSET("pool", cst[:, 2:3], 0.0, [cstb])
        cT, cTb = P.sb([128, 8], F32, "cT")
        cond, condb = P.sb([128, 8], F32, "cond")
        adab, adabb = P.sb([128, 2, 48], F32, "adab")
        nmg, nmgb = P.sb([128, 2, 8], F32, "nmg")
        nfg, nfgb = P.sb([128, 2, 8], F32, "nfg")
        P.dma("sp", cT[:], IN("cT"), writes=[cTb])
        P.dma("sp", adab[:], IN("ada_bT"), writes=[adabb])
        P.dma("sp", nmg[:], IN("nmgT"), writes=[nmgb])
        P.dma("sp", nfg[:], IN("nfgT"), writes=[nfgb])
        ACT(cond[:], cT[:], AF.Silu, [cTb], [condb])
        wts = [P.sb([128, 8, 768], F32, "adaw") for _ in range(2)]
        pm, pmb = bank(0)
        gi = 0
        for layer in range(2):
            awv = IN("ada_w")[layer].rearrange("(k p) c -> p k c", p=128)
            for grp in range(8):
                wt, wtb = wts[gi % 2]
                for k in range(8):
                    P.dma("sp" if k % 2 == 0 else QB, wt[:, k, :], awv[:, k, grp * 768:(grp + 1) * 768], writes=[wtb])
                for mi in range(6):
                    m = grp * 6 + mi
                    col = layer * 48 + m
                    for k in range(8):
                        MM(pm[:, col:col + 1], wt[:, k, mi * 128:(mi + 1) * 128], cond[:, k:k + 1],
                           k == 0, k == 7, [wtb, condb], [pmb])
                gi += 1
            TT("dve", modT[:, layer, :], pm[:, layer * 48:(layer + 1) * 48], adab[:, layer, :], ALU.add,
               [pmb, adabb], [modb])
            STT("dve", gsm[:, layer, :], modT[:, layer, 8:16], 1.0, nmg[:, layer, :], ALU.add, ALU.mult,
                [modb, nmgb], [gsmb])
            STT("dve", gsf[:, layer, :], modT[:, layer, 32:40], 1.0, nfg[:, layer, :], ALU.add, ALU.mult,
                [modb, nfgb], [gsfb])
        phase_end()

    def norm_block(xt, xtb, hT, hTb, gs, shift, Rv, T, nbanks):
        n = xt.shape[2]
        sq, sqb = T["sq"]
        tmp, tmpb = T["tmp"]
        rs, rsb = T["rs"]
        pn, pnb = bank(nbanks)
        ACT(sq[:, :, 0:n], xt[:, :, :], AF.Square, [xtb], [sqb])
        for k in range(8):
            MM(pn[:, 0:n], ones_bf[:], sq[:, k, 0:n], k == 0, k == 7, [onesb, sqb], [pnb])
        ACT(rs[:, 0:n], pn[:, 0:n], AF.Sqrt, [pnb, cstb], [rsb], scale=1.0 / D, bias=cst[:, 0:1])
        RECIP(rs[:, 0:n], rs[:, 0:n], [rsb], [rsb])
        TT("dve", tmp[:, :, 0:n], xt[:, :, :], rs[:, 0:n].unsqueeze(1).to_broadcast([128, 8, n]), ALU.mult,
           [xtb, rsb], [tmpb])
        for k in range(8):
            if shift is None:
                ACT(hT[:, k, 0:n], tmp[:, k, 0:n], AF.Identity, [tmpb] + Rv, [hTb], scale=gs[:, k:k + 1],
                    bias=cst[:, 2:3])
            else:
                ACT(hT[:, k, 0:n], tmp[:, k, 0:n], AF.Identity, [tmpb] + Rv, [hTb], scale=gs[:, k:k + 1],
                    bias=shift[:, k:k + 1])

    def load_w(dst, dstb, src, kt, q="pool"):
        v = src.rearrange("(k p) c -> p k c", p=128)
        for k in range(kt):
            P.dma(q, dst[:, k, :], v[:, k, :], writes=[dstb])

    with contextlib.ExitStack() as ps:
        P.stack = ps
        XB = 512
        xins = [P.sb([128, 4, D], F32, "xin") for _ in range(2)]
        xTs = [P.sb([128, 8, XB], F32, "xT") for _ in range(2)]
        for blk in range(SEG // XB):
            t0 = blk * XB
            xin, xinb = xins[blk % 2]
            xT, xTb = xTs[blk % 2]
            P.dma("sp", xin[:], IN("x")[t0:t0 + XB, :].rearrange("(tt p) f -> p tt f", p=128), writes=[xinb])
            for k in range(8):
                pk, pkb = bank(k % 4)
                for tt in range(4):
                    P.op("pe", (lambda o, i: lambda e: e.transpose(o, i, ident[:]))(
                        pk[:, tt * 128:(tt + 1) * 128], xin[:, tt, k * 128:(k + 1) * 128]),
                        reads=[xinb, identb], writes=[pkb])
                CP("act" if k % 2 == 0 else "dve", xT[:, k, :], pk[:, :], [pkb], [xTb])
            P.dma("sp", XTv[:, :, t0:t0 + XB], xT[:], reads=[xTb])
        phase_end()

    def ffn_phase(l):
        FB = 512
        with contextlib.ExitStack() as ps:
            P.stack = ps
            w1, w1b = P.sb([128, 8, FH], BF16, "w1")
            w3, w3b = P.sb([128, 8, FH], BF16, "w3")
            w2, w2b = P.sb([128, JH, D], BF16, "w2")
            load_w(w1, w1b, IN("ffn_w1")[l], 8)
            load_w(w3, w3b, IN("ffn_w3")[l], 8)
            load_w(w2, w2b, IN("ffn_w2")[l], JH)
            xt, xtb0 = P.sb([128, 8, FB], F32, "xt")
            xtbs = [P.buf("xtk%d" % k) for k in range(8)]
            sq, sqb = P.sb([128, 8, FB], BF16, "sq")
            rs, rsb = P.sb([128, FB], F32, "rs")
            hT, hTb = P.sb([128, 8, FB], BF16, "hT")
            aT, aTb = P.sb([128, JH, FB], BF16, "aT")
            sils = [P.sb([128, FB], F32, "sil") for _ in range(2)]
            for blk in range(SEG // FB):
                t0 = blk * FB
                for k in range(8):
                    P.dma("sp", xt[:, k, :], XTv[:, k, t0:t0 + FB], writes=[xtbs[k]])
                pn, pnb = bank(6)
                ACT(sq[:], xt[:], AF.Square, xtbs, [sqb])
                for k in range(8):
                    MM(pn[:, 0:FB], ones_bf[:], sq[:, k, :], k == 0, k == 7, [onesb, sqb], [pnb])
                ACT(rs[:], pn[:, 0:FB], AF.Sqrt, [pnb, cstb], [rsb], scale=1.0 / D, bias=cst[:, 0:1])
                RECIP(rs[:], rs[:], [rsb], [rsb])
                TT("dve", aT[:, 0:8, :], xt[:], rs[:].unsqueeze(1).to_broadcast([128, 8, FB]), ALU.mult, xtbs + [rsb], [aTb])
                for k in range(8):
                    ACT(hT[:, k, :], aT[:, k, :], AF.Identity, [aTb, gsfb, modb], [hTb], scale=gsf[:, l, k:k + 1],
                        bias=modT[:, l, 24 + k:25 + k])
                for j in range(JH):
                    pa, pab = bank(j % 2)
                    pc, pcb = bank(2 + j % 2)
                    for k in range(8):
                        MM(pa[:, 0:FB], w1[:, k, j * 128:(j + 1) * 128], hT[:, k, :], k == 0, k == 7, [w1b, hTb], [pab])
                    for k in range(8):
                        MM(pc[:, 0:FB], w3[:, k, j * 128:(j + 1) * 128], hT[:, k, :], k == 0, k == 7, [w3b, hTb], [pcb])
                    sil, silb = sils[j % 2]
                    ACT(sil[:], pa[:, 0:FB], AF.Silu, [pab], [silb])
                    TT("dve", aT[:, j, :], sil[:], pc[:, 0:FB], ALU.mult, [silb, pcb], [aTb])
                for m in range(8):
                    po, pob = bank(4 + m % 2)
                    for j in range(JH):
                        MM(po[:, 0:FB], w2[:, j, m * 128:(m + 1) * 128], aT[:, j, :], j == 0, j == JH - 1, [w2b, aTb], [pob])
                    STT("dve", xt[:, m, :], po[:, 0:FB], modT[:, l, 40 + m:41 + m], xt[:, m, :], ALU.mult, ALU.add,
                        [pob, modb, xtbs[m]], [xtbs[m]])
                    P.dma("sp", XTv[:, m, t0:t0 + FB], xt[:, m, :], reads=[xtbs[m]])
            phase_end()

    def out_phase():
        with contextlib.ExitStack() as ps:
            P.stack = ps
            XB = 512
            xts = [P.sb([128, 8, XB], F32, "xt") for _ in range(2)]
            T = {"sq": P.sb([128, 8, XB], BF16, "sq"), "tmp": P.sb([128, 8, XB], F32, "tmp"),
                 "rs": P.sb([128, XB], F32, "rs")}
            yT, yTb = P.sb([128, 8, XB], F32, "yT")
            oks = [P.sb([128, 4, D], F32, "otok") for _ in range(2)]
            for blk in range(SEG // XB):
                t0 = blk * XB
                xt, xtb = xts[blk % 2]
                otok, otokb = oks[blk % 2]
                P.dma("sp", xt[:], XTv[:, :, t0:t0 + XB], writes=[xtb])
                norm_block(xt, xtb, yT, yTb, fng, None, [fngb, cstb], T, 6)
                ei = 0
                for tt in range(4):
                    for half in range(2):
                        pk, pkb = bank((tt * 2 + half) % 4)
                        for k4 in range(4):
                            k = half * 4 + k4
                            P.op("pe", (lambda o, i: lambda e: e.transpose(o, i, ident[:]))(
                                pk[:, k4 * 128:(k4 + 1) * 128], yT[:, k, tt * 128:(tt + 1) * 128]),
                                reads=[yTb, identb], writes=[pkb])
                        CP("act" if ei % 2 == 0 else "dve", otok[:, tt, half * 512:(half + 1) * 512], pk[:, :], [pkb], [otokb])
                        ei += 1
                P.dma("sp", out[t0:t0 + XB, :].rearrange("(tt p) f -> p tt f", p=128), otok[:], reads=[otokb])
            phase_end()

    def TR(o, i, R, W):
        P.op("pe", lambda e: e.transpose(o, i, ident[:]), reads=R + [identb], writes=W)

    def drive(gens):
        gens = [g for g in gens if g is not None]
        while gens:
            nxt = []
            for g in gens:
                try:
                    next(g)
                    nxt.append(g)
                except StopIteration:
                    pass
            gens = nxt

    def run(gen):
        for _ in gen:
            pass

    def recur_front(Tt, heads, dv, cs, full, trb=0):
        lgX, lgXb = Tt["lgX"]
        k32, k32b = Tt["k32"]
        cum, cumb = Tt["cum"]
        E, Eb = Tt["E"]
        kd32, kd32b = Tt["kd32"]
        ext, extb = Tt["ext"]
        kl32, kl32b = Tt["kl32"]
        for hd in range(heads):
            SCAN(cum[:, hd, :], rmask[:, 0:NB], lgX[:, hd, :], 0.0, [rmaskb, lgXb], [cumb])
            yield
        ACT(E[:], cum[:], AF.Exp, [cumb], [Eb], scale=-cs)
        yield
        TT("dve", kd32[:], k32[:], E[:], ALU.mult, [k32b, Eb], [kd32b])
        cum4 = cum[:].rearrange("p h (c j) -> p h c j", j=64)
        ACT(ext[:], cum4[:, :, :, 63], AF.Exp, [cumb], [extb], scale=cs)
        yield
        TT("pool", kl32[:].rearrange("p h (c j) -> p h c j", j=64),
           kd32[:].rearrange("p h (c j) -> p h c j", j=64),
           ext[:].unsqueeze(3).to_broadcast([128, heads, NCH, 64]), ALU.mult, [kd32b, extb], [kl32b])
        yield
        if full:
            qact, qactb = Tt["qact"]
            qd, qdb = Tt["qd"]
            kd, kdb = Tt["kd"]
            ACT(E[:], cum[:], AF.Exp, [cumb], [Eb], scale=cs)
            yield
            TT("dve", qd[:], qact[:], E[:], ALU.mult, [qactb, Eb], [qdb])
            CP("pool", kd[:], kd32[:], [kd32b], [kdb])
            yield
        for tt in range(NTT):
            tsl = slice(tt * 128, (tt + 1) * 128)
            kltok, kltokb = Tt["kltok"][tt]
            bT, bTb = bank(trb)
            for hd in range(heads):
                TR(bT[:, hd * 128:(hd + 1) * 128], kl32[:, hd, tsl], [kl32b], [bTb])
            CP("act", kltok[:], bT[:, 0:heads * 128], [bTb], [kltokb])
            yield
            if full:
                sT, sTb = Tt["sT"][tt]
                bS, bSb = bank(1)
                for hd in range(heads):
                    MM(bS[:, hd * 128:(hd + 1) * 128], kd[:, hd, tsl], qd[:, hd, tsl], True, True, [kdb, qdb], [bSb])
                TT("dve", sT[:], bS[:, 0:heads * 128].rearrange("p (h i) -> p h i", i=128),
                   smask[:].unsqueeze(1).to_broadcast([128, heads, 128]), ALU.mult, [bSb, smaskb], [sTb])
                yield

    def recur_back(Tt, St, heads, dv, cs, full):
        VH = dv // 128
        ext, extb = Tt["ext"]
        if full:
            qd, qdb = Tt["qd"]
            oT, oTb = Tt["oT"]
        for tt in range(NTT):
            tsl = slice(tt * 128, (tt + 1) * 128)
            kltok, kltokb = Tt["kltok"][tt]
            vtok, vtokb = Tt["vtok"][tt]
            if full:
                sT, sTb = Tt["sT"][tt]
            for c in range(2):
                chunk = tt * 2 + c
                csl = slice(c * 64, (c + 1) * 64)
                for hd in range(heads):
                    if full:
                        cur, curb = St["Sbf"][St["sidx"][hd]][hd]
                        for vh in range(VH):
                            bO, bOb = bank(2 + vh)
                            oreg = bO[:, hd * 128 + c * 64:hd * 128 + c * 64 + 64]
                            MM(oreg, cur[:, vh * 128:(vh + 1) * 128], qd[:, hd, tt * 128 + c * 64:tt * 128 + c * 64 + 64],
                               True, False, [curb, qdb], [bOb])
                            MM(oreg, vtok[csl, hd * dv + vh * 128:hd * dv + vh * 128 + 128], sT[csl, hd, csl],
                               False, True, [vtokb, sTb], [bOb])
                    bD, bDb = bank(4 + St["di"] % 2)
                    St["di"] += 1
                    S32, S32b = St["S32"][hd]
                    MM(bD[:, 0:dv], kltok[csl, hd * 128:(hd + 1) * 128], vtok[csl, hd * dv:(hd + 1) * dv], True, True,
                       [kltokb, vtokb], [bDb])
                    STT("dve", S32[:], S32[:], ext[:, hd, chunk:chunk + 1], bD[:, 0:dv], ALU.mult, ALU.add,
                        [S32b, extb, bDb], [S32b])
                    if full:
                        nxt = 1 - St["sidx"][hd]
                        nb_, nbb = St["Sbf"][nxt][hd]
                        CP("pool", nb_[:], S32[:], [S32b], [nbb])
                        St["sidx"][hd] = nxt
                    yield
            if full:
                oT4 = oT[:].rearrange("p (h v) n -> p h v n", v=VH)
                for vh in range(VH):
                    bO, bOb = bank(2 + vh)
                    CP("act", oT4[:, :, vh, tsl], bO[:, 0:heads * 128].rearrange("p (h i) -> p h i", i=128), [bOb], [oTb])
                yield

    def recur(Tt, St, heads, dv, cs, full):
        run(recur_front(Tt, heads, dv, cs, full))
        run(recur_back(Tt, St, heads, dv, cs, full))

    def post_norm(*a, **k):
        run(post_norm_g(*a, **k))

    def post_norm_g(Tt, heads, dv, ng, ngb, nb0=6):
        VH = dv // 128
        oT, oTb = Tt["oT"]
        sqo, sqob = Tt["sqo"]
        rso, rsob = Tt["rso"]
        tn, tnb = Tt["tn"]
        gact, gactb = Tt["gact"]
        if "yg_ap" in Tt:
            yg_ap, ygb = Tt["yg_ap"]
        else:
            yg_ap, ygb = Tt["yg"][0][:], Tt["yg"][1]
        ACT(sqo[:], oT[:], AF.Square, [oTb], [sqob])
        yield
        for hd in range(heads):
            bN, bNb = bank(nb0 + hd // 2)
            reg = bN[:, (hd % 2) * NB:(hd % 2) * NB + NB]
            for vh in range(VH):
                MM(reg, ones_bf[:], sqo[:, hd * VH + vh, :], vh == 0, vh == VH - 1, [onesb, sqob], [bNb])
        for pb in range(heads // 2):
            bN, bNb = bank(nb0 + pb)
            ACT(rso[:, 2 * pb:2 * pb + 2, :], bN[:, 0:2 * NB].rearrange("p (h n) -> p h n", n=NB), AF.Sqrt,
                [bNb, cstb], [rsob], scale=1.0 / dv, bias=cst[:, 0:1])
        yield
        RECIP(rso[:], rso[:], [rsob], [rsob])
        oT4 = oT[:].rearrange("p (h v) n -> p h v n", v=VH)
        tn4 = tn[:].rearrange("p (h v) n -> p h v n", v=VH)
        g4 = gact[:].rearrange("p (h v) n -> p h v n", v=VH)
        y4 = yg_ap.rearrange("p (h v) n -> p h v n", v=VH)
        TT("dve", tn4, oT4, rso[:].unsqueeze(2).to_broadcast([128, heads, VH, NB]), ALU.mult, [oTb, rsob], [tnb])
        yield
        for vh in range(VH):
            STT("dve", y4[:, :, vh, :], tn4[:, :, vh, :], ng[:, vh:vh + 1], g4[:, :, vh, :], ALU.mult, ALU.mult,
                [tnb, ngb, gactb], [ygb])

    def exchange(ex_i, ex_o, pack_fn, unpack_fn, width):
        with contextlib.ExitStack() as es:
            P.stack = es
            exs, exsb = P.sb([128, EXW], F32, "exs")
            exg, exgb = P.sb([128, 4, EXW], F32, "exg")
            MEMSET("pool", exs[:], 0.0, [exsb])
            pack_fn(exs, exsb)
            CW = 256
            nchunk = (EXW + CW - 1) // CW
            for ci_ in range(nchunk):
                c0 = ci_ * CW
                cw = min(CW, EXW - c0)
                di_ = nc.dram_tensor("exi_%d_%d" % (ex_i, ci_), [128, cw], F32)
                do_ = nc.dram_tensor("exo_%d_%d" % (ex_i, ci_), [4 * 128, cw], F32)
                eob = P.buf("exout")
                tk = P.dma("sp", di_.ap(), exs[:, c0:c0 + cw], reads=[exsb])
                cc_count[0] += 1
                P.raw("pool", (lambda a, b: lambda e: e.collective_compute(
                    "AllGather", ALU.bypass, replica_groups=GROUPS, ins=[a.ap()], outs=[b.ap()]))(di_, do_),
                    cc_sem, 1, extra=[tk])
                eob.w = (cc_sem, cc_count[0])
                P.dma("sp", exg[:, :, c0:c0 + cw], do_.ap().rearrange("(r p) c -> p r c", p=128), reads=[eob], writes=[exgb])
            unpack_fn(exg, exgb)
            phase_end()

    def combine_states(exg, exgb, St, heads, dv, cs, off_d):
        W = heads * dv
        Dr, Drb = P.sb([128, 3, heads], F32, "Dr")
        acc, accb = P.sb([128, heads, dv], F32, "acc")
        new, newb = P.sb([128, heads, dv], F32, "new")
        ACT(Dr[:], exg[:, 0:3, off_d:off_d + heads], AF.Exp, [exgb], [Drb], scale=cs)
        MEMSET("pool", acc[:], 0.0, [accb])
        for r in range(3):
            TT("dve", new[:], acc[:], Dr[:, r, :].unsqueeze(2).to_broadcast([128, heads, dv]), ALU.mult,
               [accb, Drb], [newb])
            TT("dve", new[:], new[:], exg[:, r, 0:W].rearrange("p (h v) -> p h v", v=dv), ALU.add, [newb, exgb], [newb])
            TT("dve", new[:], new[:], acc[:], ALU.subtract, [newb, accb], [newb])
            STT("dve", acc[:], new[:], segm[:, r:r + 1], acc[:], ALU.mult, ALU.add, [newb, segmb, accb], [accb])
        for hd in range(heads):
            S32, S32b = St["S32"][hd]
            CP("dve", S32[:], acc[:, hd, :], [accb], [S32b])
            sb0, sb0b = St["Sbf"][0][hd]
            CP("pool", sb0[:], acc[:, hd, :], [accb], [sb0b])
            St["sidx"][hd] = 0

    def new_state(heads, dv):
        St = {"S32": [P.sb([128, dv], F32, "S32") for _ in range(heads)],
              "Sbf": [[P.sb([128, dv], BF16, "Sbf") for _ in range(heads)] for _ in range(2)],
              "sidx": [0] * heads, "di": 0}
        for hd in range(heads):
            MEMSET("pool", St["S32"][hd][0][:], 0.0, [St["S32"][hd][1]])
        St["dsum"] = P.sb([128, heads], F32, "dsum")
        MEMSET("pool", St["dsum"][0][:], 0.0, [St["dsum"][1]])
        return St

    def recur_tiles(heads, dv, full):
        VH = dv // 128
        Tt = {k: P.sb([128, heads, NB], F32, k) for k in ("lgX", "k32", "cum", "E", "kd32", "kl32")}
        Tt["ext"] = P.sb([128, heads, NCH], F32, "ext")
        Tt["kltok"] = [P.sb([128, heads * 128], BF16, "kltok") for _ in range(NTT)]
        Tt["vtok"] = [P.sb([128, heads * dv], BF16, "vtok") for _ in range(NTT)]
        Tt["red"] = P.sb([128, heads], F32, "red")
        if full:
            Tt["qact"] = P.sb([128, heads, NB], F32, "qact")
            Tt["qd"] = P.sb([128, heads, NB], BF16, "qd")
            Tt["kd"] = P.sb([128, heads, NB], BF16, "kd")
            Tt["sT"] = [P.sb([128, heads, 128], BF16, "sT") for _ in range(NTT)]
            Tt["oT"] = P.sb([128, heads * VH, NB], F32, "oT")
            Tt["sqo"] = P.sb([128, heads * VH, NB], BF16, "sqo")
            Tt["rso"] = P.sb([128, heads, NB], F32, "rso")
            Tt["tn"] = P.sb([128, heads * VH, NB], F32, "tn")
            Tt["gact"] = P.sb([128, heads * VH, NB], BF16, "gact")
            Tt["yg"] = P.sb([128, heads * VH, NB], BF16, "yg")
        return Tt

    def alt_tiles(base, heads, dv, full):
        alt = dict(base)
        alt["ext"] = P.sb([128, heads, NCH], F32, "ext")
        alt["kltok"] = [P.sb([128, heads * 128], BF16, "kltok") for _ in range(NTT)]
        alt["vtok"] = [P.sb([128, heads * dv], BF16, "vtok") for _ in range(NTT)]
        if full:
            VH = dv // 128
            alt["qd"] = P.sb([128, heads, NB], BF16, "qd")
            alt["sT"] = [P.sb([128, heads, 128], BF16, "sT") for _ in range(NTT)]
            alt["gact"] = P.sb([128, heads * VH, NB], BF16, "gact")
        return [base, alt]

    def proj_fm(*a):
        run(proj_fm_g(*a))

    def proj_tm(*a):
        run(proj_tm_g(*a))

    def proj_fm_g(dst_fn, w, wb, col0, ntile, hT, hTb, bsel):
        for i in range(ntile):
            pb_, pbb = bank(6 + (bsel[0] % 2))
            bsel[0] += 1
            for k in range(8):
                MM(pb_[:, 0:NB], w[:, k, col0 + i * 128:col0 + (i + 1) * 128], hT[:, k, :], k == 0, k == 7, [wb, hTb], [pbb])
            dst_fn(i, pb_[:, 0:NB], pbb)
            yield

    def proj_tm_g(vtoks, w, wb, col0, width, hT, hTb, bsel):
        for tt in range(NTT):
            vt, vtb = vtoks[tt]
            for c0 in range(0, width, 512):
                cw = min(512, width - c0)
                pb_, pbb = bank(6 + (bsel[0] % 2))
                bsel[0] += 1
                for k in range(8):
                    MM(pb_[:, 0:cw], hT[:, k, tt * 128:(tt + 1) * 128], w[:, k, col0 + c0:col0 + c0 + cw], k == 0, k == 7,
                       [hTb, wb], [pbb])
                CP("act", vt[:, c0:c0 + cw], pb_[:, 0:cw], [pbb], [vtb])
                yield

    def out_proj_residual(*a, **k):
        run(out_proj_residual_g(*a, **k))

    def out_proj_residual_g(wout, woutb, ycat, ycatb, xt, xtb, l, bsel, ob0=6):
        for m in range(8):
            po, pob = bank(ob0 + (bsel[0] % 2))
            bsel[0] += 1
            for c in range(8):
                MM(po[:, 0:NB], wout[:, c, m * 128:(m + 1) * 128], ycat[:, c, :], c == 0, c == 7, [woutb, ycatb], [pob])
            STT("dve", xt[:, m, :], po[:, 0:NB], modT[:, l, 16 + m:17 + m], xt[:, m, :], ALU.mult, ALU.add,
                [pob, modb, xtb], [xtb])
            yield

    def gla_layer():
        l, heads, dv, cs = 1, 4, 256, -1.0 / 16.0
        with contextlib.ExitStack() as ls:
            P.stack = ls
            win, winb = P.sb([128, 8, 3072], BF16, "win1")
            wout, woutb = P.sb([128, 8, D], BF16, "wout1")
            wa1, wa1b = P.sb([128, 8, 16], BF16, "wa1")
            wa2, wa2b = P.sb([16, 512], F32, "wa2")
            nba, nbab = P.sb([128, 4], F32, "nba")
            ng, ngb = P.sb([128, 2], F32, "ng1")
            load_w(win, winb, IN("od_w_in"), 8)
            load_w(wout, woutb, IN("od_w_out"), 8)
            load_w(wa1, wa1b, IN("od_w_a1"), 8)
            P.dma("sp", wa2[:], IN("od_w_a2"), writes=[wa2b])
            P.dma("sp", nba[:], IN("od_baT"), writes=[nbab])
            P.dma("sp", ng[:], IN("gla_ngT"), writes=[ngb])
            TS("dve", nba[:], nba[:], -1.0, None, ALU.mult, None, [nbab], [nbab])
            St = new_state(heads, dv)

            def run_pass(full):
                with contextlib.ExitStack() as bs:
                    P.stack = bs
                    Tts = alt_tiles(recur_tiles(heads, dv, full), heads, dv, full)
                    xts = [P.sb([128, 8, NB], F32, "xt") for _ in range(2)]
                    hTs = [P.sb([128, 8, NB], BF16, "hT") for _ in range(2)]
                    NT = {"sq": P.sb([128, 8, NB], BF16, "sq"), "tmp": P.sb([128, 8, NB], F32, "tmp"),
                          "rs": P.sb([128, NB], F32, "rs")} if not full else None
                    a1s, a1sb = P.sb([16, NB], F32, "a1s")
                    e1, e1b = P.sb([128, NB], F32, "e1")
                    bsel = [0]
                    bselB = [0]

                    def front(blk):
                        t0 = blk * NB
                        Tt = Tts[blk % 2]
                        xt, xtb = xts[blk % 2]
                        hT, hTb = hTs[blk % 2]
                        P.dma("sp", xt[:], XTv[:, :, t0:t0 + NB], writes=[xtb])
                        if not full:
                            norm_block(xt, xtb, hT, hTb, gsm[:, l, :], modT[:, l, 0:8], [gsmb, modb], NT, 6)
                            P.dma("sp", HTv[:, :, t0:t0 + NB], hT[:], reads=[hTb])
                        else:
                            P.dma("sp", hT[:], HTv[:, :, t0:t0 + NB], writes=[hTb])
                        yield
                        k32, k32b = Tt["k32"]
                        lgX, lgXb = Tt["lgX"]
                        yield from proj_fm_g(lambda i, p_, pb2: CP("act", k32[:, i, :], p_, [pb2], [k32b]), win, winb, 512, 4, hT, hTb, bsel)
                        pg, pgb = bank(6 + (bsel[0] % 2))
                        bsel[0] += 1
                        for k in range(8):
                            MM(pg[0:16, 0:NB], wa1[:, k, :], hT[:, k, :], k == 0, k == 7, [wa1b, hTb], [pgb])
                        CP("act", a1s[:], pg[0:16, 0:NB], [pgb], [a1sb])
                        yield
                        for hd in range(4):
                            pz, pzb = bank(6 + (bsel[0] % 2))
                            bsel[0] += 1
                            MM(pz[:, 0:NB], wa2[:, hd * 128:(hd + 1) * 128], a1s[:], True, True, [wa2b, a1sb], [pzb])
                            ACT(e1[:], pz[:, 0:NB], AF.Exp, [pzb, nbab], [e1b], scale=-1.0, bias=nba[:, hd:hd + 1])
                            ACT(lgX[:, hd, :], e1[:], AF.Ln, [e1b, cstb], [lgXb], scale=1.0, bias=cst[:, 1:2])
                            yield
                        yield from proj_tm_g(Tt["vtok"], win, winb, 1024, 1024, hT, hTb, bsel)
                        if full:
                            qact, qactb = Tt["qact"]
                            gact, gactb = Tt["gact"]
                            yield from proj_fm_g(lambda i, p_, pb2: ACT(qact[:, i, :], p_, AF.Copy, [pb2], [qactb], scale=128.0 ** -0.5),
                                                 win, winb, 0, 4, hT, hTb, bsel)
                            yield from proj_fm_g(lambda i, p_, pb2: ACT(gact[:, i, :], p_, AF.Silu, [pb2], [gactb]),
                                                 win, winb, 2048, 8, hT, hTb, bsel)
                        else:
                            red, redb = Tt["red"]
                            dsum, dsumb = St["dsum"]
                            P.op("dve", lambda e: e.tensor_reduce(out=red[:], in_=lgX[:], axis=mybir.AxisListType.X, op=ALU.add),
                                 reads=[lgXb], writes=[redb])
                            TT("dve", dsum[:], dsum[:], red[:], ALU.add, [dsumb, redb], [dsumb])
                            yield
                        yield from recur_front(Tt, heads, dv, cs, full)

                    def back(blk):
                        t0 = blk * NB
                        Tt = Tts[blk % 2]
                        xt, xtb = xts[blk % 2]
                        yield from recur_back(Tt, St, heads, dv, cs, full)
                        if full:
                            yield from post_norm_g(Tt, heads, dv, ng, ngb, nb0=4)
                            yg, ygb = Tt["yg"]
                            yield from out_proj_residual_g(wout, woutb, yg, ygb, xt, xtb, l, bselB, ob0=4)
                            P.dma("sp", XTv[:, :, t0:t0 + NB], xt[:], reads=[xtb])
                            yield

                    run(front(0))
                    for blk in range(NBLK):
                        drive([back(blk), front(blk + 1) if blk + 1 < NBLK else None])
                    phase_end()

            run_pass(False)

            def pack(exs, exsb):
                for hd in range(heads):
                    S32, S32b = St["S32"][hd]
                    CP("dve", exs[:, hd * dv:(hd + 1) * dv], S32[:], [S32b], [exsb])
                dsum, dsumb = St["dsum"]
                CP("dve", exs[:, 1024:1024 + heads], dsum[:], [dsumb], [exsb])

            exchange(1, None, pack, lambda exg, exgb: combine_states(exg, exgb, St, heads, dv, cs, 1024), EXW)
            run_pass(True)

    def load_w_cols(dst, dstb, src, kt, c0, c1, d0=0, q="pool"):
        v = src.rearrange("(k p) c -> p k c", p=128)
        for k in range(kt):
            P.dma(q, dst[:, k, d0:d0 + (c1 - c0)], v[:, k, c0:c1], writes=[dstb])

    def sincos(ang_ap, shape, c_out, s_out, Rb, Wb, tag):
        y, yb = P.sb(shape, F32, "y" + tag)
        ki, kib = P.sb(shape, mybir.dt.int32, "ki" + tag)
        kf, kfb = P.sb(shape, F32, "kf" + tag)
        for which, dst in ((0, s_out), (1, c_out)):
            TS("dve", y[:], ang_ap, 1.0 / TWO_PI, 0.25 * which, ALU.mult, ALU.add, Rb, [yb])
            CP("dve", ki[:], y[:], [yb], [kib])
            CP("dve", kf[:], ki[:], [kib], [kfb])
            TT("dve", y[:], y[:], kf[:], ALU.subtract, [yb, kfb], [yb])
            TS("dve", kf[:], y[:], 0.5, None, ALU.is_gt, None, [yb], [kfb])
            TT("dve", y[:], y[:], kf[:], ALU.subtract, [yb, kfb], [yb])
            TS("dve", kf[:], y[:], -0.5, None, ALU.is_lt, None, [yb], [kfb])
            TT("dve", y[:], y[:], kf[:], ALU.add, [yb, kfb], [yb])
            ACT(dst, y[:], AF.Sin, [yb], Wb, scale=TWO_PI)
        return y, yb

    def ev_layer():
        l, heads, dv, cs = 0, 4, 128, 1.0
        NFR = NB // FR
        with contextlib.ExitStack() as ls:
            P.stack = ls
            wout, woutb = P.sb([128, 8, D], BF16, "wout0")
            wglu, wglub = P.sb([128, 4, 512], BF16, "wglu")
            load_w(wout, woutb, IN("ev_w_out"), 8)
            load_w(wglu, wglub, IN("s5_w_glu"), 4)
            lb, lbb = P.sb([128, 4], F32, "lb")
            oml, omlb = P.sb([128, 4], F32, "oml")
            noml, nomlb = P.sb([128, 4], F32, "noml")
            ngh, nghb = P.sb([128, 1], F32, "ngh")
            dsk, dskb = P.sb([128, 4], F32, "dsk")
            bglu, bglub = P.sb([128, 4], F32, "bglu")
            P.dma("sp", ngh[:], IN("hg_ngT"), writes=[nghb])
            P.dma("sp", dsk[:], IN("s5_dT"), writes=[dskb])
            P.dma("sp", bglu[:], IN("s5_bgluT"), writes=[bglub])
            Bp = [P.sb([128, 4, 8, 64], BF16, "Bp%d" % i) for i in range(2)]
            Cp = [P.sb([128, 16, 128], BF16, "Cp%d" % i) for i in range(2)]
            cosT, cosb = P.sb([128, 16, FR + 1], F32, "cosT")
            sinT, sinb = P.sb([128, 16, FR + 1], F32, "sinT")
            rtab, rtabb = P.sb([128, 16, FR], F32, "rtab")
            cF, cFb = P.sb([128, 16], F32, "cF")
            sF, sFb = P.sb([128, 16], F32, "sF")
            Lre, Lreb = P.sb([128, 16], F32, "Lre")
            Lim, Limb = P.sb([128, 16], F32, "Lim")
            car = [P.sb([128, 2, 4], F32, "car%d" % i) for i in range(4)]
            cari = [P.buf("cari%d" % i) for i in range(4)]
            for i in range(4):
                MEMSET("pool", car[i][0][:], 0.0, [car[i][1], cari[i]])
            St = new_state(heads, dv)

            with contextlib.ExitStack() as ss:
                P.stack = ss
                lbr, lbrb = P.sb([128, 3, 4], F32, "lbr")
                P.dma("sp", lbr[:], IN("hg_lbT"), writes=[lbrb])
                ACT(lbr[:], lbr[:], AF.Exp, [lbrb], [lbrb])
                TT("dve", oml[:], lbr[:, 0, :], lbr[:, 1, :], ALU.add, [lbrb], [omlb])
                TT("dve", oml[:], oml[:], lbr[:, 2, :], ALU.add, [omlb, lbrb], [omlb])
                RECIP(oml[:], oml[:], [omlb], [omlb])
                TT("dve", lb[:], lbr[:, 0, :], oml[:], ALU.mult, [lbrb, omlb], [lbb])
                TS("dve", oml[:], lb[:], -1.0, 1.0, ALU.mult, ALU.add, [lbb], [omlb])
                TS("dve", noml[:], oml[:], -1.0, None, ALU.mult, None, [omlb], [nomlb])

                def sbt(shape, name):
                    return P.sb(shape, F32, name)
                G3 = [128, 4, 64]
                lre, lreb = sbt(G3, "lre"); lim, limb = sbt(G3, "lim")
                bre, breb = sbt(G3, "bre"); bim, bimb = sbt(G3, "bim")
                dl, dlb = sbt([128, 4], "dl")
                P.dma("sp", lre[:], IN("lamre_gh"), writes=[lreb])
                P.dma("sp", lim[:], IN("lamim_gh"), writes=[limb])
                P.dma("sp", bre[:], IN("bre_gh"), writes=[breb])
                P.dma("sp", bim[:], IN("bim_gh"), writes=[bimb])
                P.dma("sp", dl[:], IN("lstep_gh"), writes=[dlb])
                ACT(dl[:], dl[:], AF.Exp, [dlb], [dlb])
                a_, a_b = sbt(G3, "a_"); th, thb = sbt(G3, "th")
                dl3 = dl[:].unsqueeze(2).to_broadcast(G3)
                TT("dve", a_[:], lre[:], dl3, ALU.mult, [lreb, dlb], [a_b])
                TT("dve", th[:], lim[:], dl3, ALU.mult, [limb, dlb], [thb])
                r_, r_b = sbt(G3, "r_")
                ACT(r_[:], a_[:], AF.Exp, [a_b], [r_b])
                cth, cthb = sbt(G3, "cth"); sth, sthb = sbt(G3, "sth")
                sincos(th[:], G3, cth[:], sth[:], [thb], [cthb, sthb], "g")
                er, erb = sbt(G3, "er"); ei, eib_ = sbt(G3, "ei")
                TT("dve", er[:], r_[:], cth[:], ALU.mult, [r_b, cthb, sthb], [erb])
                TS("dve", er[:], er[:], -1.0, None, ALU.add, None, [erb], [erb])
                TT("dve", ei[:], r_[:], sth[:], ALU.mult, [r_b, cthb, sthb], [eib_])
                den, denb = sbt(G3, "den"); tq, tqb = sbt(G3, "tq")
                TT("dve", den[:], lre[:], lre[:], ALU.mult, [lreb], [denb])
                TT("dve", tq[:], lim[:], lim[:], ALU.mult, [limb], [tqb])
                TT("dve", den[:], den[:], tq[:], ALU.add, [denb, tqb], [denb])
                RECIP(den[:], den[:], [denb], [denb])
                cr, crb = sbt(G3, "cr"); ci, cib = sbt(G3, "ci")
                TT("dve", cr[:], er[:], lre[:], ALU.mult, [erb, lreb], [crb])
                TT("dve", tq[:], ei[:], lim[:], ALU.mult, [eib_, limb], [tqb])
                TT("dve", cr[:], cr[:], tq[:], ALU.add, [crb, tqb], [crb])
                TT("dve", cr[:], cr[:], den[:], ALU.mult, [crb, denb], [crb])
                TT("dve", ci[:], ei[:], lre[:], ALU.mult, [eib_, lreb], [cib])
                TT("dve", tq[:], er[:], lim[:], ALU.mult, [erb, limb], [tqb])
                TT("dve", ci[:], ci[:], tq[:], ALU.subtract, [cib, tqb], [cib])
                TT("dve", ci[:], ci[:], den[:], ALU.mult, [cib, denb], [cib])
                bbr, bbrb = sbt(G3, "bbr"); bbi, bbib = sbt(G3, "bbi")
                TT("dve", bbr[:], cr[:], bre[:], ALU.mult, [crb, breb], [bbrb])
                TT("dve", tq[:], ci[:], bim[:], ALU.mult, [cib, bimb], [tqb])
                TT("dve", bbr[:], bbr[:], tq[:], ALU.subtract, [bbrb, tqb], [bbrb])
                TT("dve", bbi[:], cr[:], bim[:], ALU.mult, [crb, bimb], [bbib])
                TT("dve", tq[:], ci[:], bre[:], ALU.mult, [cib, breb], [tqb])
                TT("dve", bbi[:], bbi[:], tq[:], ALU.add, [bbib, tqb], [bbib])
                eye8, eye8b = sbt([128, 8], "eye8")
                P.dma("sp", eye8[:], IN("eye8"), writes=[eye8b])
                for ct in range(4):
                    for src_, srcb, dsti in ((bbr, bbrb, 0), (bbi, bbib, 1)):
                        TT("dve", Bp[dsti][0][:, ct, :, :], src_[:, ct, :].unsqueeze(1).to_broadcast([128, 8, 64]),
                           eye8[:].unsqueeze(2).to_broadcast([128, 8, 64]), ALU.mult, [srcb, eye8b], [Bp[dsti][1]])
                P2 = [128, 16]
                lrp, lrpb = sbt(P2, "lrp"); lip, lipb = sbt(P2, "lip"); dlp, dlpb = sbt(P2, "dlp")
                P.dma("sp", lrp[:], IN("lamre_pp"), writes=[lrpb])
                P.dma("sp", lip[:], IN("lamim_pp"), writes=[lipb])
                P.dma("sp", dlp[:], IN("lstep_pp"), writes=[dlpb])
                ACT(dlp[:], dlp[:], AF.Exp, [dlpb], [dlpb])
                ap_, ap_b = sbt(P2, "ap_"); thp, thpb = sbt(P2, "thp"); rp, rpb = sbt(P2, "rp")
                TT("dve", ap_[:], lrp[:], dlp[:], ALU.mult, [lrpb, dlpb], [ap_b])
                TT("dve", thp[:], lip[:], dlp[:], ALU.mult, [lipb, dlpb], [thpb])
                ACT(rp[:], ap_[:], AF.Exp, [ap_b], [rpb])
                CP("dve", rtab[:], rp[:].unsqueeze(2).to_broadcast([128, 16, FR]), [rpb], [rtabb])
                yk, ykb = sbt(P2, "yk"); kip, kipb = P.sb(P2, mybir.dt.int32, "kip"); kfp, kfpb = sbt(P2, "kfp")
                TS("dve", yk[:], thp[:], 1.0 / TWO_PI, None, ALU.mult, None, [thpb], [ykb])
                CP("dve", kip[:], yk[:], [ykb], [kipb])
                CP("dve", kfp[:], kip[:], [kipb], [kfpb])
                TT("dve", yk[:], yk[:], kfp[:], ALU.subtract, [ykb, kfpb], [ykb])
                TS("dve", thp[:], yk[:], TWO_PI, None, ALU.mult, None, [ykb], [thpb])
                jv, jvb = sbt([128, FR + 1], "jv")
                P.dma("sp", jv[:], IN("jvec"), writes=[jvb])
                A3 = [128, 16, FR + 1]
                ang, angb = sbt(A3, "ang")
                TT("dve", ang[:], thp[:].unsqueeze(2).to_broadcast(A3), jv[:].unsqueeze(1).to_broadcast(A3), ALU.mult,
                   [thpb, jvb], [angb])
                sincos(ang[:], A3, cosT[:], sinT[:], [angb], [cosb, sinb], "p")
                CP("dve", cF[:], cosT[:, :, FR], [cosb, sinb], [cFb])
                CP("dve", sF[:], sinT[:, :, FR], [cosb, sinb], [sFb])
                c2, c2b = sbt(P2, "c2"); s2, s2b = sbt(P2, "s2"); u1, u1b = sbt(P2, "u1"); u2, u2b = sbt(P2, "u2")
                CP("dve", c2[:], cF[:], [cFb], [c2b])
                CP("dve", s2[:], sF[:], [sFb], [s2b])
                n = FR
                while n < SEG:
                    TT("dve", u1[:], c2[:], c2[:], ALU.mult, [c2b], [u1b])
                    TT("dve", u2[:], s2[:], s2[:], ALU.mult, [s2b], [u2b])
                    TT("dve", u2[:], u1[:], u2[:], ALU.subtract, [u1b, u2b], [u2b])
                    TT("dve", u1[:], c2[:], s2[:], ALU.mult, [c2b, s2b], [u1b])
                    TS("dve", s2[:], u1[:], 2.0, None, ALU.mult, None, [u1b], [s2b])
                    CP("dve", c2[:], u2[:], [u2b], [c2b])
                    n *= 2
                ACT(u1[:], ap_[:], AF.Exp, [ap_b], [u1b], scale=float(SEG))
                TT("dve", Lre[:], u1[:], c2[:], ALU.mult, [u1b, c2b], [Lreb])
                TT("dve", Lim[:], u1[:], s2[:], ALU.mult, [u1b, s2b], [Limb])
                crp, crpb = sbt([128, 16, 16], "crp"); cip, cipb = sbt([128, 16, 16], "cip")
                cm2, cm2b = sbt([128, 4, 8], "cm2")
                P.dma("sp", crp[:], IN("cre_pp"), writes=[crpb])
                P.dma("sp", cip[:], IN("cim_pp"), writes=[cipb])
                P.dma("sp", cm2[:], IN("cmask2"), writes=[cm2b])
                C4 = [128, 4, 8, 16]
                for ct in range(4):
                    psl = slice(ct * 4, ct * 4 + 4)
                    TT("dve", Cp[0][0][:, psl, :].rearrange("p q (g h) -> p q g h", h=16),
                       crp[:, psl, :].unsqueeze(2).to_broadcast(C4), cm2[:].unsqueeze(3).to_broadcast(C4), ALU.mult,
                       [crpb, cm2b], [Cp[0][1]])
                    TT("dve", Cp[1][0][:, psl, :].rearrange("p q (g h) -> p q g h", h=16),
                       cip[:, psl, :].unsqueeze(2).to_broadcast(C4), cm2[:].unsqueeze(3).to_broadcast(C4), ALU.mult,
                       [cipb, cm2b], [Cp[1][1]])
                    TS("dve", Cp[1][0][:, psl, :], Cp[1][0][:, psl, :], -1.0, None, ALU.mult, None, [Cp[1][1]], [Cp[1][1]])
                phase_end()

            if EVSTOP <= 1:
                return

            def s5_block(*a):
                run(s5_block_g(*a))

            def s5_block_g(S5T, uT, uTb, full, gi):
                t = S5T
                for f in range(NFR):
                    fsl = slice(f * FR, (f + 1) * FR)
                    for ct in range(4):
                        bR, bRb = bank((gi[0] % 2) * 2)
                        bI, bIb = bank((gi[0] % 2) * 2 + 1)
                        gi[0] += 1
                        for q in range(4):
                            MM(bR[:, q * FR:(q + 1) * FR], Bp[0][0][:, ct, 2 * q:2 * q + 2, :].rearrange("p a b -> p (a b)"),
                               uT[:, ct, fsl], True, True, [Bp[0][1], uTb], [bRb])
                        for q in range(4):
                            MM(bI[:, q * FR:(q + 1) * FR], Bp[1][0][:, ct, 2 * q:2 * q + 2, :].rearrange("p a b -> p (a b)"),
                               uT[:, ct, fsl], True, True, [Bp[1][1], uTb], [bIb])
                        bre3 = bR[:, 0:4 * FR].rearrange("p (i j) -> p i j", j=FR)
                        bim3 = bI[:, 0:4 * FR].rearrange("p (i j) -> p i j", j=FR)
                        cs3 = cosT[:, ct * 4:ct * 4 + 4, 0:FR]
                        sn3 = sinT[:, ct * 4:ct * 4 + 4, 0:FR]
                        t1, t1b = t["t1"]; t2, t2b = t["t2"]; t3, t3b = t["t3"]; t4, t4b = t["t4"]
                        dre, dreb = t["dre"]; dim, dimb = t["dim"]; wre, wreb = t["wre"]; wim, wimb = t["wim"]
                        TT("dve", t1[:], bre3, cs3, ALU.mult, [bRb, cosb], [t1b])
                        TT("dve", t2[:], bim3, sn3, ALU.mult, [bIb, sinb], [t2b])
                        TT("pool", dre[:], t1[:], t2[:], ALU.add, [t1b, t2b], [dreb])
                        yield
                        TT("dve", t3[:], bim3, cs3, ALU.mult, [bIb, cosb], [t3b])
                        TT("dve", t4[:], bre3, sn3, ALU.mult, [bRb, sinb], [t4b])
                        TT("pool", dim[:], t3[:], t4[:], ALU.subtract, [t3b, t4b], [dimb])
                        yield
                        cr_, crb_ = car[ct]
                        cib_ = cari[ct]
                        for q in range(4):
                            pi_ = ct * 4 + q
                            SCAN(wre[:, q, :], rtab[:, pi_, :], dre[:, q, :], cr_[:, 0, q:q + 1], [rtabb, dreb, crb_], [wreb])
                            SCAN(wim[:, q, :], rtab[:, pi_, :], dim[:, q, :], cr_[:, 1, q:q + 1], [rtabb, dimb, cib_], [wimb])
                            yield
                        k1, k1b = t["k1"]; k2, k2b = t["k2"]
                        cFh = cF[:, ct * 4:ct * 4 + 4]
                        sFh = sF[:, ct * 4:ct * 4 + 4]
                        wlr = wre[:, :, FR - 1]
                        wli = wim[:, :, FR - 1]
                        k3, k3b = t["k3"]; k4, k4b = t["k4"]
                        TT("dve", k1[:], wlr, cFh, ALU.mult, [wreb, cFb], [k1b])
                        TT("dve", k2[:], wli, sFh, ALU.mult, [wimb, sFb], [k2b])
                        TT("dve", k3[:], wlr, sFh, ALU.mult, [wreb, sFb], [k3b])
                        TT("dve", k4[:], wli, cFh, ALU.mult, [wimb, cFb], [k4b])
                        TT("dve", cr_[:, 0, :], k1[:], k2[:], ALU.subtract, [k1b, k2b], [crb_])
                        TT("dve", cr_[:, 1, :], k3[:], k4[:], ALU.add, [k3b, k4b], [cib_])
                        yield
                        if full:
                            xre, xreb = t["xre"]; xim, ximb = t["xim"]
                            TT("dve", t1[:], wre[:], cs3, ALU.mult, [wreb, cosb], [t1b])
                            TT("dve", t2[:], wim[:], sn3, ALU.mult, [wimb, sinb], [t2b])
                            TT("dve", xre[:], t1[:], t2[:], ALU.subtract, [t1b, t2b], [xreb])
                            TT("dve", t3[:], wre[:], sn3, ALU.mult, [wreb, sinb], [t3b])
                            TT("dve", t4[:], wim[:], cs3, ALU.mult, [wimb, cosb], [t4b])
                            TT("dve", xim[:], t3[:], t4[:], ALU.add, [t3b, t4b], [ximb])
                            bY, bYb = bank(4 + ct // 2)
                            reg = bY[:, (ct % 2) * NB + f * FR:(ct % 2) * NB + (f + 1) * FR]
                            for q in range(4):
                                pi_ = ct * 4 + q
                                MM(reg, Cp[0][0][:, pi_, :], xre[:, q, :], q == 0, False, [Cp[0][1], xreb], [bYb])
                                MM(reg, Cp[1][0][:, pi_, :], xim[:, q, :], False, q == 3, [Cp[1][1], ximb], [bYb])

            def s5_tiles(full):
                t = {k: P.sb([128, 4, FR], F32, k) for k in ("t1", "t2", "t3", "t4", "dre", "dim", "wre", "wim")}
                t["k1"] = P.sb([128, 4], F32, "k1")
                t["k2"] = P.sb([128, 4], F32, "k2")
                t["k3"] = P.sb([128, 4], F32, "k3")
                t["k4"] = P.sb([128, 4], F32, "k4")
                if full:
                    t["xre"] = P.sb([128, 4, FR], BF16, "xre")
                    t["xim"] = P.sb([128, 4, FR], BF16, "xim")
                return t

            def hg_kpath(Tt, win, winb, cF_, cI_, hT, hTb, bsel, sg, sgb):
                k32, k32b = Tt["k32"]
                lgX, lgXb = Tt["lgX"]
                proj_fm(lambda i, p_, pb2: ACT(sg[:, i, :], p_, AF.Sigmoid, [pb2], [sgb]), win, winb, cF_, 4, hT, hTb, bsel)
                for hd in range(4):
                    ACT(lgX[:, hd, :], sg[:, hd, :], AF.Ln, [sgb, omlb, lbb], [lgXb], scale=oml[:, hd:hd + 1], bias=lb[:, hd:hd + 1])
                    TS("dve", k32[:, hd, :], sg[:, hd, :], noml[:, hd:hd + 1], oml[:, hd:hd + 1], ALU.mult, ALU.add,
                       [sgb, nomlb, omlb], [k32b])
                proj_tm(Tt["vtok"], win, winb, cI_, 512, hT, hTb, bsel)

            with contextlib.ExitStack() as bs:
                P.stack = bs
                win, winb = P.sb([128, 8, 1536], BF16, "winA")
                load_w_cols(win, winb, IN("ev_w_in"), 8, 0, 512, 0)
                load_w_cols(win, winb, IN("ev_w_in"), 8, 1024, 2048, 512)
                Tt = recur_tiles(heads, dv, False)
                S5T = s5_tiles(False)
                xts = [P.sb([128, 8, NB], F32, "xt") for _ in range(2)]
                hTs = [P.sb([128, 8, NB], BF16, "hT") for _ in range(2)]
                NT = {"sq": P.sb([128, 8, NB], BF16, "sq"), "tmp": P.sb([128, 8, NB], F32, "tmp"),
                      "rs": P.sb([128, NB], F32, "rs")}
                sg, sgb = P.sb([128, 4, NB], F32, "sg")
                uT, uTb = P.sb([128, 4, NB], BF16, "uT")
                bsel = [0]
                gi = [0]
                for blk in range(NBLK):
                    t0 = blk * NB
                    xt, xtb = xts[blk % 2]
                    hT, hTb = hTs[blk % 2]
                    P.dma("sp", xt[:], XTv[:, :, t0:t0 + NB], writes=[xtb])
                    norm_block(xt, xtb, hT, hTb, gsm[:, l, :], modT[:, l, 0:8], [gsmb, modb], NT, 6)
                    P.dma("sp", HTv[:, :, t0:t0 + NB], hT[:], reads=[hTb])
                    proj_fm(lambda i, p_, pb2: CP("act", uT[:, i, :], p_, [pb2], [uTb]), win, winb, 0, 4, hT, hTb, bsel)
                    def hg_stream():
                        hg_kpath(Tt, win, winb, 512, 1024, hT, hTb, bsel, sg, sgb)
                        yield
                        yield from recur_front(Tt, heads, dv, cs, False, trb=6)
                        yield from recur_back(Tt, St, heads, dv, cs, False)
                    drive([s5_block_g(S5T, uT, uTb, False, gi), hg_stream()])
                    red, redb = Tt["red"]
                    lgX, lgXb = Tt["lgX"]
                    dsum, dsumb = St["dsum"]
                    P.op("dve", lambda e: e.tensor_reduce(out=red[:], in_=lgX[:], axis=mybir.AxisListType.X, op=ALU.add),
                         reads=[lgXb], writes=[redb])
                    TT("dve", dsum[:], dsum[:], red[:], ALU.add, [dsumb, redb], [dsumb])
                phase_end()

            if EVSTOP <= 2:
                return

            def pack(exs, exsb):
                for hd in range(heads):
                    S32, S32b = St["S32"][hd]
                    CP("dve", exs[:, hd * dv:(hd + 1) * dv], S32[:], [S32b], [exsb])
                dsum, dsumb = St["dsum"]
                CP("dve", exs[:, 1024:1024 + heads], dsum[:], [dsumb], [exsb])
                for ct in range(4):
                    CP("dve", exs[:, 1028 + ct * 8:1028 + ct * 8 + 8].rearrange("p (a b) -> p a b", b=4), car[ct][0][:],
                       [car[ct][1], cari[ct]], [exsb])

            def unpack(exg, exgb):
                combine_states(exg, exgb, St, heads, dv, cs, 1024)
                ar, arb = P.sb([128, 16], F32, "ar"); ai, aib = P.sb([128, 16], F32, "ai")
                nr, nrb = P.sb([128, 16], F32, "nr"); ni, nib = P.sb([128, 16], F32, "ni")
                v1, v1b = P.sb([128, 16], F32, "v1")
                MEMSET("pool", ar[:], 0.0, [arb])
                MEMSET("pool", ai[:], 0.0, [aib])
                for r in range(3):
                    cl = exg[:, r, 1028:1060].rearrange("p (c a b) -> p c a b", a=2, b=4)
                    clr = cl[:, :, 0, :]
                    cli = cl[:, :, 1, :]
                    ar4 = ar[:].rearrange("p (c b) -> p c b", b=4); ai4 = ai[:].rearrange("p (c b) -> p c b", b=4)
                    nr4 = nr[:].rearrange("p (c b) -> p c b", b=4); ni4 = ni[:].rearrange("p (c b) -> p c b", b=4)
                    TT("dve", nr[:], Lre[:], ar[:], ALU.mult, [Lreb, arb], [nrb])
                    TT("dve", v1[:], Lim[:], ai[:], ALU.mult, [Limb, aib], [v1b])
                    TT("dve", nr[:], nr[:], v1[:], ALU.subtract, [nrb, v1b], [nrb])
                    TT("dve", nr4, nr4, clr, ALU.add, [nrb, exgb], [nrb])
                    TT("dve", ni[:], Lre[:], ai[:], ALU.mult, [Lreb, aib], [nib])
                    TT("dve", v1[:], Lim[:], ar[:], ALU.mult, [Limb, arb], [v1b])
                    TT("dve", ni[:], ni[:], v1[:], ALU.add, [nib, v1b], [nib])
                    TT("dve", ni4, ni4, cli, ALU.add, [nib, exgb], [nib])
                    TT("dve", nr[:], nr[:], ar[:], ALU.subtract, [nrb, arb], [nrb])
                    TT("dve", ni[:], ni[:], ai[:], ALU.subtract, [nib, aib], [nib])
                    STT("dve", ar[:], nr[:], segm[:, r:r + 1], ar[:], ALU.mult, ALU.add, [nrb, segmb, arb], [arb])
                    STT("dve", ai[:], ni[:], segm[:, r:r + 1], ai[:], ALU.mult, ALU.add, [nib, segmb, aib], [aib])
                for ct in range(4):
                    CP("dve", car[ct][0][:, 0, :], ar[:, ct * 4:ct * 4 + 4], [arb], [car[ct][1]])
                    CP("dve", car[ct][0][:, 1, :], ai[:, ct * 4:ct * 4 + 4], [aib], [cari[ct]])

            exchange(0, None, pack, unpack, EXW)
            if EVSTOP <= 3:
                return

            with contextlib.ExitStack() as bs:
                P.stack = bs
                win, winb = P.sb([128, 8, 512], BF16, "winU")
                load_w_cols(win, winb, IN("ev_w_in"), 8, 0, 512, 0)
                S5T = s5_tiles(True)
                hTs = [P.sb([128, 8, NB], BF16, "hT") for _ in range(2)]
                uT, uTb = P.sb([128, 4, NB], BF16, "uT")
                u32s = [P.sb([128, 4, NB], F32, "u32") for _ in range(2)]
                uTs = [P.sb([128, 4, NB], BF16, "uTb") for _ in range(2)]
                y32, y32b = P.sb([128, 4, NB], F32, "y32")
                g1, g1b = P.sb([128, 4, NB], F32, "g1")
                g2, g2b = P.sb([128, 4, NB], F32, "g2")
                gl, glb = P.sb([128, 4, NB], F32, "gl")
                glh, glhb = P.sb([128, 4, NB], BF16, "glh")
                sg2s = [P.sb([128, NB], F32, "sg2") for _ in range(2)]
                yas = [P.sb([128, 4, NB], BF16, "ya") for _ in range(2)]
                bsel = [0]
                bselB = [0]
                gi = [0]

                def front(blk):
                    t0 = blk * NB
                    hT, hTb = hTs[blk % 2]
                    u32, u32b = u32s[blk % 2]
                    uT_, uT_b = uTs[blk % 2]
                    P.dma("sp", hT[:], HTv[:, :, t0:t0 + NB], writes=[hTb])

                    def put_u(i, p_, pb2):
                        CP("act", u32[:, i, :], p_, [pb2], [u32b])
                        CP("dve", uT_[:, i, :], u32[:, i, :], [u32b], [uT_b])
                    yield from proj_fm_g(put_u, win, winb, 0, 4, hT, hTb, bsel)
                    yield from s5_block_g(S5T, uT_, uT_b, True, gi)

                def back(blk):
                    t0 = blk * NB
                    u32, u32b = u32s[blk % 2]
                    ya, yab = yas[blk % 2]
                    for ct in range(4):
                        bY, bYb = bank(4 + ct // 2)
                        STT("dve", y32[:, ct, :], u32[:, ct, :], dsk[:, ct:ct + 1], bY[:, (ct % 2) * NB:(ct % 2) * NB + NB],
                            ALU.mult, ALU.add, [u32b, dskb, bYb], [y32b])
                    yield
                    TT("pool", g1[:], y32[:], y32[:], ALU.mult, [y32b], [g1b])
                    yield
                    TS("dve", g1[:], g1[:], 0.044715, 1.0, ALU.mult, ALU.add, [g1b], [g1b])
                    yield
                    TT("pool", g2[:], g1[:], y32[:], ALU.mult, [g1b, y32b], [g2b])
                    yield
                    ACT(g1[:], g2[:], AF.Sigmoid, [g2b], [g1b], scale=1.5957691216057308)
                    yield
                    TT("dve", gl[:], y32[:], g1[:], ALU.mult, [y32b, g1b], [glb])
                    CP("act", glh[:], gl[:], [glb], [glhb])
                    yield
                    for co in range(4):
                        pz, pzb = bank(6 + (bselB[0] % 2))
                        bselB[0] += 1
                        for ct in range(4):
                            MM(pz[:, 0:NB], wglu[:, ct, co * 128:(co + 1) * 128], glh[:, ct, :], ct == 0, ct == 3,
                               [wglub, glhb], [pzb])
                        s2_, s2b_ = sg2s[co % 2]
                        ACT(s2_[:], pz[:, 0:NB], AF.Sigmoid, [pzb, bglub], [s2b_], bias=bglu[:, co:co + 1])
                        TT("dve", ya[:, co, :], gl[:, co, :], s2_[:], ALU.mult, [glb, s2b_], [yab])
                        yield
                    P.dma("sp", YAv[:, :, t0:t0 + NB], ya[:], reads=[yab])
                    yield

                run(front(0))
                for blk in range(NBLK):
                    drive([back(blk), front(blk + 1) if blk + 1 < NBLK else None])
                phase_end()

            if EVSTOP <= 4:
                return

            with contextlib.ExitStack() as bs:
                P.stack = bs
                win, winb = P.sb([128, 8, 2048], BF16, "winB")
                load_w_cols(win, winb, IN("ev_w_in"), 8, 512, 2560, 0)
                Tts = alt_tiles(recur_tiles(heads, dv, True), heads, dv, True)
                xts = [P.sb([128, 8, NB], F32, "xt") for _ in range(2)]
                hTs = [P.sb([128, 8, NB], BF16, "hT") for _ in range(2)]
                ycs = [P.sb([128, 8, NB], BF16, "ycat") for _ in range(2)]
                sg, sgb = P.sb([128, 4, NB], F32, "sg")
                bsel = [0]
                bselB = [0]

                def front(blk):
                    t0 = blk * NB
                    Tt = Tts[blk % 2]
                    xt, xtb = xts[blk % 2]
                    hT, hTb = hTs[blk % 2]
                    yc, ycb = ycs[blk % 2]
                    P.dma("sp", xt[:], XTv[:, :, t0:t0 + NB], writes=[xtb])
                    P.dma("sp", hT[:], HTv[:, :, t0:t0 + NB], writes=[hTb])
                    P.dma("sp", yc[:, 0:4, :], YAv[:, :, t0:t0 + NB], writes=[ycb])
                    yield
                    k32, k32b = Tt["k32"]
                    lgX, lgXb = Tt["lgX"]
                    yield from proj_fm_g(lambda i, p_, pb2: ACT(sg[:, i, :], p_, AF.Sigmoid, [pb2], [sgb]), win, winb, 512, 4, hT, hTb, bsel)
                    for hd in range(4):
                        ACT(lgX[:, hd, :], sg[:, hd, :], AF.Ln, [sgb, omlb, lbb], [lgXb], scale=oml[:, hd:hd + 1], bias=lb[:, hd:hd + 1])
                        TS("dve", k32[:, hd, :], sg[:, hd, :], noml[:, hd:hd + 1], oml[:, hd:hd + 1], ALU.mult, ALU.add,
                           [sgb, nomlb, omlb], [k32b])
                        yield
                    yield from proj_tm_g(Tt["vtok"], win, winb, 1024, 512, hT, hTb, bsel)
                    qact, qactb = Tt["qact"]
                    gact, gactb = Tt["gact"]
                    yield from proj_fm_g(lambda i, p_, pb2: ACT(qact[:, i, :], p_, AF.Silu, [pb2], [qactb]), win, winb, 0, 4, hT, hTb, bsel)
                    yield from proj_fm_g(lambda i, p_, pb2: ACT(gact[:, i, :], p_, AF.Silu, [pb2], [gactb]), win, winb, 1536, 4, hT, hTb, bsel)
                    yield from recur_front(Tt, heads, dv, cs, True)

                def back(blk):
                    t0 = blk * NB
                    Tt = dict(Tts[blk % 2])
                    xt, xtb = xts[blk % 2]
                    yc, ycb = ycs[blk % 2]
                    yield from recur_back(Tt, St, heads, dv, cs, True)
                    Tt["yg_ap"] = (yc[:, 4:8, :], ycb)
                    yield from post_norm_g(Tt, heads, dv, ngh, nghb, nb0=4)
                    yield from out_proj_residual_g(wout, woutb, yc, ycb, xt, xtb, l, bselB, ob0=4)
                    P.dma("sp", XTv[:, :, t0:t0 + NB], xt[:], reads=[xtb])
                    yield

                run(front(0))
                for blk in range(NBLK):
                    drive([back(blk), front(blk + 1) if blk + 1 < NBLK else None])
                phase_end()

    for st in stages:
        if st == "F0":
            ffn_phase(0)
        elif st == "F1":
            ffn_phase(1)
        elif st == "L1":
            gla_layer()
        elif st == "L0":
            ev_layer()
    out_phase()
    main.close()
    nc._used_inputs = set(_ins.keys())
    return nc


def _colT(v, nt):
    return np.ascontiguousarray(np.asarray(v, np.float32).reshape(nt, 128).T)


def host_inputs(inp, SEG, core):
    b, s = core // 4, core % 4
    f32 = np.float32
    g = lambda k: np.asarray(inp[k], f32)
    m = {}
    m["x"] = np.ascontiguousarray(g("x")[b, s * SEG:(s + 1) * SEG, :])
    m["cT"] = _colT(g("c")[b], 8)
    m["ada_w"] = g("ada_w")
    m["ada_bT"] = np.ascontiguousarray(g("ada_b").reshape(2, 48, 128).transpose(2, 0, 1))
    m["nmgT"] = np.ascontiguousarray(g("norm_mix_g").reshape(2, 8, 128).transpose(2, 0, 1))
    m["nfgT"] = np.ascontiguousarray(g("norm_ffn_g").reshape(2, 8, 128).transpose(2, 0, 1))
    m["fngT"] = _colT(g("final_norm_g"), 8)
    m["ev_w_in"] = g("ev_w_in")[0]
    m["ev_w_out"] = g("ev_w_out")[0]
    G, Pn, H = 32, 64, 16
    lre, lim, lst = g("s5_lam_re")[0], g("s5_lam_im")[0], g("s5_log_step")[0]
    rep = lambda a: np.ascontiguousarray(np.repeat(a, H, axis=0).reshape(4, 128, -1).transpose(1, 0, 2))
    m["lamre_gh"] = rep(lre)
    m["lamim_gh"] = rep(lim)
    m["lstep_gh"] = np.ascontiguousarray(rep(lst[:, None])[:, :, 0])
    tb = lambda a: np.ascontiguousarray(a.transpose(0, 2, 1).reshape(4, 128, Pn).transpose(1, 0, 2))
    m["bre_gh"] = tb(g("s5_b_re")[0])
    m["bim_gh"] = tb(g("s5_b_im")[0])
    pp = lambda a: np.ascontiguousarray(a.reshape(16, 2, Pn).transpose(1, 2, 0).reshape(128, 16))
    m["lamre_pp"] = pp(lre)
    m["lamim_pp"] = pp(lim)
    m["lstep_pp"] = pp(np.repeat(lst[:, None], Pn, axis=1))
    cp = lambda a: np.ascontiguousarray(a.reshape(16, 2, H, Pn).transpose(1, 3, 0, 2).reshape(128, 16, H))
    m["cre_pp"] = cp(g("s5_c_re")[0])
    m["cim_pp"] = cp(g("s5_c_im")[0])
    m["s5_dT"] = _colT(g("s5_d")[0], 4)
    m["s5_w_glu"] = g("s5_w_glu")[0]
    m["s5_bgluT"] = _colT(g("s5_b_glu")[0], 4)
    m["hg_lbT"] = np.ascontiguousarray(g("hg_lb_logits").reshape(3, 4, 128).transpose(2, 0, 1))
    m["hg_ngT"] = np.ascontiguousarray(g("hg_norm_g")[0].reshape(128, 1))
    m["od_w_in"] = g("od_w_in")[0]
    m["od_w_a1"] = g("od_w_a1")[0]
    m["od_w_a2"] = g("od_w_a2")[0]
    m["od_baT"] = _colT(g("od_b_a")[0], 4)
    m["gla_ngT"] = _colT(g("gla_norm_g")[0], 2)
    m["od_w_out"] = g("od_w_out")[0]
    m["ffn_w1"] = g("ffn_w1")
    m["ffn_w3"] = g("ffn_w3")
    m["ffn_w2"] = g("ffn_w2")
    m["ident"] = np.eye(128, dtype=f32)
    j = np.arange(128)
    m["smask"] = ((j[:, None] // 64 == j[None, :] // 64) & (j[None, :] >= j[:, None])).astype(f32)
    rm = np.ones((128, NB), f32)
    rm[:, ::64] = 0.0
    m["rmask"] = rm
    m["jvec"] = np.ascontiguousarray(np.broadcast_to(np.arange(FR + 1, dtype=f32), (128, FR + 1)))
    m["eye8"] = (j[:, None] // 16 == np.arange(8)[None, :]).astype(f32)
    g2 = j // 64
    m["cmask2"] = (np.arange(8)[None, None, :] == (2 * np.arange(4)[None, :, None] + g2[:, None, None])).astype(f32)
    m["segm"] = np.ascontiguousarray(np.broadcast_to((np.arange(3) < s).astype(f32), (128, 3)))
    return m


_NC_CACHE = {}


def kernel(**inp):
    SEG = np.asarray(inp["x"]).shape[1] // 4
    key = SEG
    if key not in _NC_CACHE:
        _NC_CACHE[key] = build(SEG)
    nc = _NC_CACHE[key]
    in_maps = [{k: v for k, v in host_inputs(inp, SEG, c).items() if k in nc._used_inputs} for c in range(8)]
    res = run_bass_kernel_spmd(nc, in_maps, core_ids=list(range(8)))
    o = np.empty((2, 4 * SEG, D), np.float32)
    for c in range(8):
        b, s = c // 4, c % 4
        o[b, s * SEG:(s + 1) * SEG, :] = res.results[c]["out"]
    return o
```
